# Optimizing a Trainium2 kernel written in Bass

```python
import math, functools
import jax, jax.numpy as jnp
from jax import lax
import numpy as np

D_MODEL = 1024
BATCH = 8
SEQ = 2048
DEPTH = 2

GRID_W = 64
CTX_LEN = 256
HEAD_DIM = 64
ROPE_THETA = 10000.0
ROPE_FREQS = HEAD_DIM // 4
Q_BLOCK = 128
A_HEADS = 8
A_KV_HEADS = 2
B_HEADS = 4
B_V_DIM = 2 * HEAD_DIM
C_HEADS = 8
C_KV_HEADS = 2
C_WINDOW = 128
D_HEADS = 8
NA_ROWS = 8
NA_COLS = 16
EVEN_SPLITS = (A_HEADS * HEAD_DIM, A_KV_HEADS * HEAD_DIM, A_KV_HEADS * HEAD_DIM,
               2 * B_HEADS * HEAD_DIM, 2 * B_HEADS * HEAD_DIM, B_HEADS * B_V_DIM)
ODD_SPLITS = (C_HEADS * HEAD_DIM, C_KV_HEADS * HEAD_DIM, C_KV_HEADS * HEAD_DIM,
              D_HEADS * HEAD_DIM, D_HEADS * HEAD_DIM, D_HEADS * HEAD_DIM)
IN_WIDTH = sum(EVEN_SPLITS)
MIX_WIDTH = A_HEADS * HEAD_DIM + B_HEADS * B_V_DIM
D_FF = 4 * D_MODEL
N_EVEN = (DEPTH + 1) // 2
N_ODD = DEPTH // 2
EPS = 1e-6
NEG = -1e30

kernel_name = "hybrid_dit_prefix_block"


def split_cols(p, sizes):
    return jnp.split(p, np.cumsum(sizes)[:-1].tolist(), axis=-1)


def rms_norm(x, g):
    xf = x.astype(jnp.float32)
    y = xf * lax.rsqrt(jnp.mean(jnp.square(xf), axis=-1, keepdims=True) + EPS)
    return (y * g.astype(jnp.float32)).astype(x.dtype)


def modulate(h, shift, scale):
    return h * (1 + scale) + shift


def lambda_init(layer):
    return 0.8 - 0.6 * math.exp(-0.3 * layer)


def axial_rope_tables(T, dtype):
    t = jnp.arange(T, dtype=jnp.int32)
    pos = jnp.stack([t // GRID_W, t % GRID_W], axis=-1).astype(jnp.float32)
    inv = ROPE_THETA ** (-jnp.arange(ROPE_FREQS, dtype=jnp.float32) / ROPE_FREQS)
    ang = pos[..., None] * inv
    return jnp.cos(ang).astype(dtype), jnp.sin(ang).astype(dtype)


def apply_axial_rope(x, cos, sin):
    xs = x.reshape(x.shape[:-1] + (2, 2, ROPE_FREQS))
    x1, x2 = xs[..., 0, :], xs[..., 1, :]
    c, s = cos[:, None], sin[:, None]
    return jnp.stack([x1 * c - x2 * s, x2 * c + x1 * s], axis=-2).reshape(x.shape)


def map_query_blocks(fn, *qs):
    B, T = qs[0].shape[:2]
    nb = T // Q_BLOCK
    blocks = tuple(jnp.moveaxis(a.reshape((B, nb, Q_BLOCK) + a.shape[2:]), 1, 0) for a in qs)
    out = lax.map(lambda args: fn(*args), blocks)
    return jnp.moveaxis(out, 0, 1).reshape((B, T) + out.shape[3:])


def gqa_attend(q, k, v):
    s = jnp.einsum('bqkgd,bskd->bkgqs', q, k).astype(jnp.float32)
    p = jax.nn.softmax(s, axis=-1).astype(v.dtype)
    return jnp.einsum('bkgqs,bskd->bqkgd', p, v)


def diff_attend(q, k, v, lam):
    s = jnp.einsum('bqhjd,bshjd->bhjqs', q, k).astype(jnp.float32)
    p = jax.nn.softmax(s, axis=-1)
    a = (p[:, :, 0] - lam * p[:, :, 1]).astype(v.dtype)
    return jnp.einsum('bhqs,bshe->bqhe', a, v)


def sink_attend(q, k, v, sink):
    s = jnp.einsum('bqkgd,bskd->bkgqs', q, k).astype(jnp.float32)
    sk = jnp.broadcast_to(sink.astype(jnp.float32)[None, :, :, None, None], s.shape[:-1] + (1,))
    p = jax.nn.softmax(jnp.concatenate([sk, s], axis=-1), axis=-1)[..., 1:].astype(v.dtype)
    return jnp.einsum('bkgqs,bskd->bqkgd', p, v)


def windowed_attend(q, k, v, k_ctx, v_ctx, sink):
    B, T = q.shape[:2]
    L = k_ctx.shape[1]
    nb = T // C_WINDOW

    def band(a):
        ap = jnp.pad(a, ((0, 0), (C_WINDOW, C_WINDOW), (0, 0), (0, 0)))
        ap = ap.reshape((B, nb + 2, C_WINDOW) + a.shape[2:])
        return jnp.concatenate([ap[:, :-2], ap[:, 1:-1], ap[:, 2:]], axis=2)

    kb, vb = band(k), band(v)
    qb = q.reshape((B, nb, C_WINDOW) + q.shape[2:])
    i = jnp.arange(C_WINDOW)[:, None]
    j = jnp.arange(3 * C_WINDOW)[None, :]
    kpos = jnp.arange(nb)[:, None, None] * C_WINDOW - C_WINDOW + j
    valid = (jnp.abs(j - C_WINDOW - i) <= C_WINDOW)[None] & (kpos >= 0) & (kpos < T)
    s_loc = jnp.einsum('bnqkgd,bnskd->bnkgqs', qb, kb).astype(jnp.float32)
    s_loc = jnp.where(valid[None, :, None, None], s_loc, NEG)
    s_ctx = jnp.einsum('bnqkgd,bskd->bnkgqs', qb, k_ctx).astype(jnp.float32)
    sk = jnp.broadcast_to(sink.astype(jnp.float32)[None, None, :, :, None, None], s_ctx.shape[:-1] + (1,))
    p = jax.nn.softmax(jnp.concatenate([sk, s_ctx, s_loc], axis=-1), axis=-1)
    p_ctx = p[..., 1:1 + L].astype(v.dtype)
    p_loc = p[..., 1 + L:].astype(v.dtype)
    o = (jnp.einsum('bnkgqs,bskd->bnqkgd', p_ctx, v_ctx)
         + jnp.einsum('bnkgqs,bnskd->bnqkgd', p_loc, vb))
    return o.reshape(q.shape)


def neighborhood_attend(q, k, v, k_ctx, v_ctx, rpb):
    B, T, H, Dh = q.shape
    rows = T // GRID_W
    kh = min(NA_ROWS, rows)
    kw = NA_COLS
    L = k_ctx.shape[1]
    r = jnp.arange(rows)
    rs = jnp.clip(r - kh // 2, 0, rows - kh)
    row_idx = rs[:, None] + jnp.arange(kh)[None, :]

    def gather_rows(a):
        return a.reshape(B, rows, GRID_W, H, Dh)[:, row_idx].reshape(B, rows, kh * GRID_W, H, Dh)

    kg, vg = gather_rows(k), gather_rows(v)
    qg = q.reshape(B, rows, GRID_W, H, Dh)
    cq = jnp.arange(GRID_W)
    cs = jnp.clip(cq - kw // 2, 0, GRID_W - kw)
    ck = jnp.tile(jnp.arange(GRID_W), kh)
    col_valid = (ck[None] >= cs[:, None]) & (ck[None] < cs[:, None] + kw)
    dr = jnp.repeat(row_idx - r[:, None], GRID_W, axis=1)
    dc = jnp.clip(ck[None] - cq[:, None], -(kw - 1), kw - 1)
    bias = rpb[:, dr[:, None, :] + NA_ROWS - 1, dc[None] + NA_COLS - 1]
    bias = jnp.moveaxis(bias, 0, 1).astype(jnp.float32)
    s_loc = jnp.einsum('brqhd,brshd->brhqs', qg, kg).astype(jnp.float32) + bias
    s_loc = jnp.where(col_valid, s_loc, NEG)
    s_ctx = jnp.einsum('brqhd,bshd->brhqs', qg, k_ctx).astype(jnp.float32)
    p = jax.nn.softmax(jnp.concatenate([s_ctx, s_loc], axis=-1), axis=-1)
    p_ctx = p[..., :L].astype(v.dtype)
    p_loc = p[..., L:].astype(v.dtype)
    o = (jnp.einsum('brhqs,bshd->brqhd', p_ctx, v_ctx)
         + jnp.einsum('brhqs,brshd->brqhd', p_loc, vg))
    return o.reshape(B, T, H * Dh)


def even_mixer(h_x, h_c, w_in, w_out, qk_g, lam_p, subln_g, lam0, cos, sin, with_ctx):
    scale = HEAD_DIM ** -0.5
    g_a = A_HEADS // A_KV_HEADS

    def project(h, rope):
        n, s = h.shape[:2]
        qa, ka, va, qb, kb, vb = split_cols(h @ w_in, EVEN_SPLITS)
        qa = rms_norm(qa.reshape(n, s, A_HEADS, HEAD_DIM), qk_g[0])
        ka = rms_norm(ka.reshape(n, s, A_KV_HEADS, HEAD_DIM), qk_g[1])
        qb = qb.reshape(n, s, 2 * B_HEADS, HEAD_DIM)
        kb = kb.reshape(n, s, 2 * B_HEADS, HEAD_DIM)
        if rope:
            qa, ka = apply_axial_rope(qa, cos, sin), apply_axial_rope(ka, cos, sin)
            qb, kb = apply_axial_rope(qb, cos, sin), apply_axial_rope(kb, cos, sin)
        return ((qa * scale).reshape(n, s, A_KV_HEADS, g_a, HEAD_DIM), ka,
                va.reshape(n, s, A_KV_HEADS, HEAD_DIM),
                (qb * scale).reshape(n, s, B_HEADS, 2, HEAD_DIM),
                kb.reshape(n, s, B_HEADS, 2, HEAD_DIM),
                vb.reshape(n, s, B_HEADS, B_V_DIM))

    lam = (jnp.exp(jnp.sum(lam_p[0] * lam_p[1]).astype(jnp.float32))
           - jnp.exp(jnp.sum(lam_p[2] * lam_p[3]).astype(jnp.float32)) + lam0)
    qa_x, ka_x, va_x, qb_x, kb_x, vb_x = project(h_x, True)
    qa_c, ka_c, va_c, qb_c, kb_c, vb_c = project(h_c, False)
    ka_all = jnp.concatenate([ka_c, ka_x], axis=1)
    va_all = jnp.concatenate([va_c, va_x], axis=1)
    kb_all = jnp.concatenate([kb_c, kb_x], axis=1)
    vb_all = jnp.concatenate([vb_c, vb_x], axis=1)
    B, T = h_x.shape[:2]
    o_a = map_query_blocks(lambda q: gqa_attend(q, ka_all, va_all), qa_x).reshape(B, T, -1)
    o_b = map_query_blocks(lambda q: diff_attend(q, kb_all, vb_all, lam), qb_x)
    o_b = (rms_norm(o_b, subln_g) * (1 - lam0)).reshape(B, T, -1)
    out_x = jnp.concatenate([o_a, o_b], axis=-1) @ w_out
    out_c = None
    if with_ctx:
        n, L = h_c.shape[:2]
        oa_c = gqa_attend(qa_c, ka_c, va_c).reshape(n, L, -1)
        ob_c = (rms_norm(diff_attend(qb_c, kb_c, vb_c, lam), subln_g) * (1 - lam0)).reshape(n, L, -1)
        out_c = jnp.concatenate([oa_c, ob_c], axis=-1) @ w_out
    return out_x, out_c


def odd_mixer(h_x, h_c, w_in, w_out, sink, rpb, cos, sin, with_ctx):
    scale = HEAD_DIM ** -0.5
    g_c = C_HEADS // C_KV_HEADS
    sink = sink.reshape(C_KV_HEADS, g_c)

    def project(h, rope):
        n, s = h.shape[:2]
        qc, kc, vc, qd, kd, vd = split_cols(h @ w_in, ODD_SPLITS)
        qc = qc.reshape(n, s, C_HEADS, HEAD_DIM)
        kc = kc.reshape(n, s, C_KV_HEADS, HEAD_DIM)
        if rope:
            qc, kc = apply_axial_rope(qc, cos, sin), apply_axial_rope(kc, cos, sin)
        return ((qc * scale).reshape(n, s, C_KV_HEADS, g_c, HEAD_DIM), kc,
                vc.reshape(n, s, C_KV_HEADS, HEAD_DIM),
                qd.reshape(n, s, D_HEADS, HEAD_DIM) * scale,
                kd.reshape(n, s, D_HEADS, HEAD_DIM),
                vd.reshape(n, s, D_HEADS, HEAD_DIM))

    qc_x, kc_x, vc_x, qd_x, kd_x, vd_x = project(h_x, True)
    qc_c, kc_c, vc_c, qd_c, kd_c, vd_c = project(h_c, False)
    B, T = h_x.shape[:2]
    o_c = windowed_attend(qc_x, kc_x, vc_x, kc_c, vc_c, sink).reshape(B, T, -1)
    o_d = neighborhood_attend(qd_x, kd_x, vd_x, kd_c, vd_c, rpb)
    out_x = jnp.concatenate([o_c, o_d], axis=-1) @ w_out
    out_c = None
    if with_ctx:
        n, L = h_c.shape[:2]
        oc_c = sink_attend(qc_c, kc_c, vc_c, sink).reshape(n, L, -1)
        od_c = gqa_attend(qd_c[:, :, :, None], kd_c, vd_c).reshape(n, L, -1)
        out_c = jnp.concatenate([oc_c, od_c], axis=-1) @ w_out
    return out_x, out_c


def squared_relu_mlp(h, w1, w2):
    return jnp.square(jax.nn.relu(h @ w1)) @ w2


def setup_inputs(seed: int = 0) -> dict:
    key = jax.random.key(seed)
    ks = jax.random.split(key, 16)
    f32 = jnp.float32

    def nrm(k, shape, s):
        return jax.random.normal(k, shape, f32) * s

    return {
        "x": nrm(ks[0], (BATCH, SEQ, D_MODEL), 1.0),
        "c": nrm(ks[1], (BATCH, D_MODEL), 1.0),
        "ctx": nrm(ks[2], (BATCH, CTX_LEN, D_MODEL), 1.0),
        "c_ctx": nrm(ks[3], (D_MODEL,), 1.0),
        "w_mod": nrm(ks[4], (DEPTH, D_MODEL, 6 * D_MODEL), D_MODEL ** -0.5),
        "b_mod": nrm(ks[5], (DEPTH, 6 * D_MODEL), 0.02),
        "norm_g": 1.0 + nrm(ks[6], (DEPTH, 4, D_MODEL), 0.02),
        "w_in": nrm(ks[7], (DEPTH, D_MODEL, IN_WIDTH), D_MODEL ** -0.5),
        "w_out": nrm(ks[8], (DEPTH, MIX_WIDTH, D_MODEL), MIX_WIDTH ** -0.5),
        "w_mlp_in": nrm(ks[9], (DEPTH, D_MODEL, D_FF), D_MODEL ** -0.5),
        "w_mlp_out": nrm(ks[10], (DEPTH, D_FF, D_MODEL), D_FF ** -0.5),
        "qk_norm_a": 1.0 + nrm(ks[11], (N_EVEN, 2, HEAD_DIM), 0.02),
        "diff_lambda": nrm(ks[12], (N_EVEN, 4, HEAD_DIM), 0.1),
        "diff_subln": 1.0 + nrm(ks[13], (N_EVEN, B_V_DIM), 0.02),
        "sink_c": nrm(ks[14], (N_ODD, C_HEADS), 0.5),
        "rpb_d": nrm(ks[15], (N_ODD, D_HEADS, 2 * NA_ROWS - 1, 2 * NA_COLS - 1), 0.1),
    }


def reference(x, c, ctx, c_ctx, w_mod, b_mod, norm_g, w_in, w_out, w_mlp_in, w_mlp_out,
              qk_norm_a, diff_lambda, diff_subln, sink_c, rpb_d):
    T = x.shape[1]
    cos, sin = axial_rope_tables(T, x.dtype)
    s_x = jax.nn.silu(c)
    s_c = jax.nn.silu(c_ctx)
    for l in range(DEPTH):
        last = l == DEPTH - 1
        mod_x = (s_x @ w_mod[l] + b_mod[l])[:, None, :]
        mod_c = s_c @ w_mod[l] + b_mod[l]
        sh1_x, sc1_x, g1_x, sh2_x, sc2_x, g2_x = jnp.split(mod_x, 6, axis=-1)
        sh1_c, sc1_c, g1_c, sh2_c, sc2_c, g2_c = jnp.split(mod_c, 6, axis=-1)
        h_x = modulate(rms_norm(x, norm_g[l, 0]), sh1_x, sc1_x)
        h_c = modulate(rms_norm(ctx, norm_g[l, 0]), sh1_c, sc1_c)
        if l % 2 == 0:
            i = l // 2
            m_x, m_c = even_mixer(h_x, h_c, w_in[l], w_out[l], qk_norm_a[i], diff_lambda[i],
                                  diff_subln[i], lambda_init(l), cos, sin, not last)
        else:
            i = l // 2
            m_x, m_c = odd_mixer(h_x, h_c, w_in[l], w_out[l], sink_c[i], rpb_d[i],
                                 cos, sin, not last)
        x = x + g1_x * rms_norm(m_x, norm_g[l, 1])
        f_x = squared_relu_mlp(modulate(rms_norm(x, norm_g[l, 2]), sh2_x, sc2_x), w_mlp_in[l], w_mlp_out[l])
        x = x + g2_x * rms_norm(f_x, norm_g[l, 3])
        if not last:
            ctx = ctx + g1_c * rms_norm(m_c, norm_g[l, 1])
            f_c = squared_relu_mlp(modulate(rms_norm(ctx, norm_g[l, 2]), sh2_c, sc2_c), w_mlp_in[l], w_mlp_out[l])
            ctx = ctx + g2_c * rms_norm(f_c, norm_g[l, 3])
    return x
```

```python
import numpy as np
import math
import os
from contextlib import ExitStack
import concourse.bass as bass
import concourse.mybir as mybir
from concourse.bass_utils import run_bass_kernel_spmd

F32 = mybir.dt.float32
BF16 = mybir.dt.bfloat16
ALU = mybir.AluOpType
AF = mybir.ActivationFunctionType
AX = mybir.AxisListType

D = 1024
T = 2048
LC = 256
NT = 18
NTOK = NT * 128
EPS = 1e-6
NEGV = -30000.0
NLAYERS = 2


class _Stop(Exception):
    pass


class Res:
    __slots__ = ("w", "r")

    def __init__(self):
        self.w = None
        self.r = {}


class Sched:
    NDMA = 24

    def __init__(self, nc, es):
        self.nc = nc
        self.eng = dict(pe=nc.tensor, dve=nc.vector, act=nc.scalar, pool=nc.gpsimd, sp=nc.sync)
        self.esem = {k: es.enter_context(nc.semaphore("s_" + k)) for k in self.eng}
        self.ecnt = {k: 0 for k in self.eng}
        self.known = {k: {} for k in self.eng}
        self.dsem = [es.enter_context(nc.semaphore("d%d" % i)) for i in range(self.NDMA)]
        self.dcnt = [0] * self.NDMA
        half = self.NDMA // 2
        self.dq = {'pool': list(range(0, half)), 'sp': list(range(half, self.NDMA))}
        self.dnext = {'pool': 0, 'sp': 0}

    def _wait(self, e, tok):
        key, val = tok
        if key == 'pe' and e == 'pe':
            return
        if self.known[e].get(key, 0) >= val:
            return
        sem = self.esem[key] if isinstance(key, str) else self.dsem[key]
        self.eng[e].wait_ge(sem, val)
        self.known[e][key] = val

    def _deps(self, e, reads, writes):
        for r in reads:
            if r.w is not None:
                self._wait(e, r.w)
        for w in writes:
            if w.w is not None:
                self._wait(e, w.w)
            for k, v in w.r.items():
                self._wait(e, (k, v))

    def _commit(self, tok, reads, writes):
        k, v = tok
        for r in reads:
            if r.r.get(k, 0) < v:
                r.r[k] = v
        for w in writes:
            w.w = tok
            w.r = {}

    def op(self, e, fn, reads=(), writes=()):
        self._deps(e, reads, writes)
        ins = fn(self.eng[e])
        self.ecnt[e] += 1
        ins.then_inc(self.esem[e], 1)
        self._commit((e, self.ecnt[e]), reads, writes)
        return ins

    def dma(self, e, out, in_, reads=(), writes=()):
        self._deps(e, reads, writes)
        q = self.dq[e]
        i = q[self.dnext[e]]
        self.dnext[e] = (self.dnext[e] + 1) % len(q)
        if self.dcnt[i] > 0:
            self._wait(e, (i, self.dcnt[i]))
        ins = self.eng[e].dma_start(out=out, in_=in_)
        self.dcnt[i] += 16
        ins.then_inc(self.dsem[i], 16)
        self._commit((i, self.dcnt[i]), reads, writes)
        return ins

    def barrier(self):
        for e in self.eng:
            for k in self.eng:
                if k != e and self.ecnt[k] > 0:
                    self._wait(e, (k, self.ecnt[k]))
            for i in range(self.NDMA):
                if self.dcnt[i] > 0:
                    self._wait(e, (i, self.dcnt[i]))


class Banks:
    def __init__(self, tensors):
        self.t = tensors
        self.res = [Res() for _ in tensors]
        self.free = list(range(len(tensors)))

    def alloc(self):
        i = self.free.pop(0)
        return i

    def release(self, i):
        self.free.append(i)


class Ring:
    def __init__(self, tiles):
        self.t = tiles
        self.res = [Res() for _ in tiles]
        self.i = 0

    def next(self):
        i = self.i
        self.i = (i + 1) % len(self.t)
        return self.t[i], self.res[i]


def lambda_init(layer):
    return 0.8 - 0.6 * math.exp(-0.3 * layer)


def build_program(layers=(0, 1), debug_stop=None):
    nc = bass.Bass("TRN2", target_bir_lowering=False)
    dt_in = lambda name, shape: nc.dram_tensor(name, shape, F32, kind="ExternalInput").ap()
    x_d = dt_in("x", [T, D])
    ctx_d = dt_in("ctx", [LC, D])
    ccol_d = dt_in("ccol", [128, 8, 2])
    wmod_d = dt_in("w_mod", [2, D, 6 * D])
    bmodcol_d = dt_in("bmod_col", [128, 2, 48])
    bmod_d = dt_in("b_mod", [2, 6 * D])
    ngcol_d = dt_in("ng_col", [128, 2, 4, 8])
    ng_d = dt_in("norm_g", [2, 4, D])
    win_d = dt_in("w_in", [2, D, 2304])
    wout_d = dt_in("w_out", [2, D, D])
    w1_d = dt_in("w_mlp_in", [2, D, 4 * D])
    w2_d = dt_in("w_mlp_out", [2, 4 * D, D])
    qkg_d = dt_in("qkg", [1, 128])
    lamp_d = dt_in("lamp", [1, 256])
    subln_d = dt_in("subln_col", [128, 1])
    sink_d = dt_in("sink", [1, 8])
    tbl_d = dt_in("tbl", [128, 8, 16, 64])
    ident_d = dt_in("ident", [128, 128])
    cos_d = dt_in("cos_t", [128, 16, 64])
    sin_d = dt_in("sin_t", [128, 16, 64])
    mprev_d = dt_in("mprev", [128, 128])
    mnext_d = dt_in("mnext", [128, 128])
    out_all = (layers[-1] != NLAYERS - 1)
    y_d = nc.dram_tensor("y", [NTOK if out_all else T, D], F32, kind="ExternalOutput").ap()

    with ExitStack() as es:
        S = Sched(nc, es)

        uid = [0]

        def sb(scope, name, shape, dt):
            uid[0] += 1
            return scope.enter_context(nc.sbuf_tensor("sb%d_%s" % (uid[0], name), shape, dt))

        X = sb(es, "X", [128, NT, D], F32)
        Xr = [Res() for _ in range(NT)]
        ident = sb(es, "ident", [128, 128], BF16)
        ones = sb(es, "ones", [128, 128], BF16)
        sT = sb(es, "sT", [128, 8, 2], BF16)
        sRep = sb(es, "sRep", [128, 2, 8, 128], BF16)
        small = sb(es, "small", [128, 64], F32)
        modcol = sb(es, "modcol", [128, 6, 8, 2], F32)
        A1 = sb(es, "A1", [128, 2, 8, 2], F32)
        SH = sb(es, "SH", [128, 2, 8, 2], F32)
        bmodcol = sb(es, "bmodcol", [128, 2, 48], F32)
        ngcol = sb(es, "ngcol", [128, 2, 4, 8], F32)
        rC = Res()
        rMod = Res()
        sc_t = [sb(es, "sc%d" % i, [128, 16], F32) for i in range(6)]
        scr = Ring(sc_t)
        ybf = Ring([sb(es, "ybf%d" % i, [128, D], BF16) for i in range(2)])
        rtmp = Ring([sb(es, "rtmp%d" % i, [128, 512], F32) for i in range(2)])
        pf = Banks([es.enter_context(nc.psum_tensor("pf%d" % i, [128, 512], F32)) for i in range(6)])
        pb = Banks([es.enter_context(nc.psum_tensor("pb%d" % i, [128, 1024], BF16)) for i in range(2)])

        S.dma('pool', ident[:], ident_d, writes=[rC])
        S.op('dve', lambda e: e.memset(ones[:], 1.0), writes=[rC])
        S.dma('sp', bmodcol[:], bmodcol_d, writes=[rC])
        S.dma('sp', ngcol[:], ngcol_d, writes=[rC])
        cc, rcc = scr.next()
        S.dma('sp', cc[:, 0:16].rearrange("p (k i) -> p k i", i=2), ccol_d, writes=[rcc])
        sf, rsf = scr.next()
        S.op('act', lambda e: e.activation(out=sf[:, 0:16], in_=cc[:, 0:16], func=AF.Silu), reads=[rcc], writes=[rsf])
        S.op('dve', lambda e: e.tensor_copy(out=sT[:], in_=sf[:, 0:16].rearrange("p (k i) -> p k i", i=2)), reads=[rsf], writes=[rC])
        for i in range(2):
            S.op('dve', lambda e, i=i: e.tensor_copy(
                out=sRep[:, i, :, :],
                in_=sf[:, 0:16].rearrange("p (k i) -> p k i", i=2)[:, :, i:i + 1].to_broadcast([128, 8, 128])),
                reads=[rsf], writes=[rC])
        S.dma('sp', X[:, 0:2, :], ctx_d.rearrange("(t p) d -> p t d", p=128), writes=Xr[0:2])
        xv = x_d.rearrange("(t p) d -> p t d", p=128)
        for i in range(4):
            S.dma('sp', X[:, 2 + 4 * i:6 + 4 * i, :], xv[:, 4 * i:4 * i + 4, :], writes=Xr[2 + 4 * i:6 + 4 * i])

        def wview(ap2d, c0, c1):
            return ap2d[:, c0:c1].rearrange("(kc p) n -> p kc n", p=128)

        def rstd_from(ss_ap, n, reads):
            m = ss_ap.shape[1]
            t, r = scr.next()
            S.op('dve', lambda e: e.tensor_scalar(out=t[:, 0:m], in0=ss_ap, scalar1=1.0 / n, scalar2=EPS,
                                                  op0=ALU.mult, op1=ALU.add), reads=reads, writes=[r])
            S.op('act', lambda e: e.activation(out=t[:, 0:m], in_=t[:, 0:m], func=AF.Sqrt), reads=[r], writes=[r])
            S.op('dve', lambda e: e.reciprocal(out=t[:, 0:m], in_=t[:, 0:m]), reads=[r], writes=[r])
            return t[:, 0:m], r

        cnt = {"ev": 0}

        def evac_engine():
            cnt["ev"] += 1
            return 'act' if cnt["ev"] % 2 == 0 else 'dve'

        def norm_mod_transpose(t, which, dst, dst_res, dst_off):
            ci = 1 if t < 2 else 0
            ss, rss = scr.next()
            y, ry = ybf.next()
            S.op('act', lambda e: e.activation(out=y[:], in_=X[:, t, :], func=AF.Square, accum_out=ss[:, 0:1]),
                 reads=[Xr[t]], writes=[ry, rss])
            rs, rrs = rstd_from(ss[:, 0:1], D, [rss])
            S.op('dve', lambda e: e.tensor_scalar_mul(out=y[:], in0=X[:, t, :], scalar1=rs[:, 0:1]), reads=[Xr[t], rrs], writes=[ry])
            b = pb.alloc()
            pt_, pr = pb.t[b], pb.res[b]
            for kc in range(8):
                S.op('pe', lambda e, kc=kc: e.transpose(pt_[:, kc * 128:(kc + 1) * 128], y[:, kc * 128:(kc + 1) * 128], ident[:]),
                     reads=[ry, rC], writes=[pr])
            for kc in range(8):
                eng = evac_engine()
                o = dst[:, kc, dst_off:dst_off + 128]
                i_ = pt_[:, kc * 128:(kc + 1) * 128]
                sc_ = A1[:, which, kc, ci:ci + 1]
                bi_ = SH[:, which, kc, ci:ci + 1]
                if eng == 'act':
                    S.op('act', lambda e, o=o, i_=i_, sc_=sc_, bi_=bi_: e.activation(out=o, in_=i_, func=AF.Identity, scale=sc_, bias=bi_),
                         reads=[pr, rMod], writes=[dst_res])
                else:
                    S.op('dve', lambda e, o=o, i_=i_, sc_=sc_, bi_=bi_: e.tensor_scalar(out=o, in0=i_, scalar1=sc_, scalar2=bi_,
                                                                                        op0=ALU.mult, op1=ALU.add),
                         reads=[pr, rMod], writes=[dst_res])
            pb.release(b)

        def residual_epilogue(t, banks, G, rG):
            ss, rss = scr.next()
            jk, rjk = ybf.next()
            for c in range(2):
                S.op('act', lambda e, c=c: e.activation(out=jk[:, c * 512:(c + 1) * 512], in_=pf.t[banks[c]][:], func=AF.Square,
                                                        accum_out=ss[:, c:c + 1]),
                     reads=[pf.res[banks[c]]], writes=[rjk, rss])
            S.op('dve', lambda e: e.tensor_tensor(out=ss[:, 2:3], in0=ss[:, 0:1], in1=ss[:, 1:2], op=ALU.add), reads=[rss], writes=[rss])
            rs, rrs = rstd_from(ss[:, 2:3], D, [rss])
            for c in range(2):
                tm, rtm = rtmp.next()
                S.op('dve', lambda e, c=c, tm=tm: e.scalar_tensor_tensor(out=tm[:], in0=pf.t[banks[c]][:], scalar=rs[:, 0:1],
                                                                         in1=G[:, c * 512:(c + 1) * 512], op0=ALU.mult, op1=ALU.mult),
                     reads=[pf.res[banks[c]], rrs, rG], writes=[rtm])
                S.op('pool', lambda e, c=c, tm=tm: e.tensor_tensor(out=X[:, t, c * 512:(c + 1) * 512], in0=X[:, t, c * 512:(c + 1) * 512],
                                                                  in1=tm[:], op=ALU.add),
                     reads=[rtm, Xr[t]], writes=[Xr[t]])

        def gate_rows(l, chunk, gidx, wbuf, rw, G, rG, which_list, Gtmp, rGtmp):
            for h in range(2):
                S.dma('pool', wbuf[:, :, h * 512:(h + 1) * 512], wview(wmod_d[l], chunk * D + h * 512, chunk * D + (h + 1) * 512), writes=[rw])
            for i in which_list:
                S.dma('sp', G[i][:], bmod_d[l:l + 1, chunk * D:(chunk + 1) * D].partition_broadcast(128), writes=[rG[i]])
            ngb, rngb = Gtmp, rGtmp
            S.dma('sp', ngb[:], ng_d[l, gidx:gidx + 1, :].partition_broadcast(128), writes=[rngb])
            for i in which_list:
                for c in range(2):
                    b = pf.alloc()
                    for kc in range(8):
                        S.op('pe', lambda e, kc=kc, c=c, b=b, i=i: e.matmul(pf.t[b][:], lhsT=sRep[:, i, kc, :], rhs=wbuf[:, kc, c * 512:(c + 1) * 512],
                                                                       start=(kc == 0), stop=(kc == 7)),
                             reads=[rC, rw], writes=[pf.res[b]])
                    S.op('dve', lambda e, c=c, b=b, i=i: e.tensor_tensor(out=G[i][:, c * 512:(c + 1) * 512], in0=pf.t[b][:],
                                                                    in1=G[i][:, c * 512:(c + 1) * 512], op=ALU.add),
                         reads=[pf.res[b], rG[i]], writes=[rG[i]])
                    pf.release(b)
                S.op('dve', lambda e, i=i: e.tensor_tensor(out=G[i][:], in0=G[i][:], in1=ngb[:], op=ALU.mult), reads=[rG[i], rngb], writes=[rG[i]])

        def run_layer(l):
            last = (l == NLAYERS - 1)
            tiles_upd = list(range(2, NT)) if last else list(range(NT))
            which_list = [0] if last else [0, 1]
            S.barrier()
            with ExitStack() as ph:
                wm = Ring([sb(ph, "wm%d" % i, [128, 8, D], BF16) for i in range(2)])
                for ch in range(0 if os.environ.get("KDBG_SKIPPROJ") else 6):
                    w_, rw_ = wm.next()
                    for h in range(2):
                        S.dma('pool', w_[:, :, h * 512:(h + 1) * 512], wview(wmod_d[l], ch * D + h * 512, ch * D + (h + 1) * 512), writes=[rw_])
                    b = pf.alloc()
                    for fc in range(8):
                        for kc in range(8):
                            S.op('pe', lambda e, fc=fc, kc=kc, w_=w_, b=b: e.matmul(pf.t[b][:, fc * 2:fc * 2 + 2], lhsT=w_[:, kc, fc * 128:(fc + 1) * 128],
                                                                                rhs=sT[:, kc, :], start=(kc == 0), stop=(kc == 7)),
                                 reads=[rw_, rC], writes=[pf.res[b]])
                    S.op('dve', lambda e, ch=ch, b=b: e.tensor_tensor(
                        out=modcol[:, ch, :, :], in0=pf.t[b][:, 0:16].rearrange("p (f i) -> p f i", i=2),
                        in1=bmodcol[:, l, ch * 8:(ch + 1) * 8].unsqueeze(2).to_broadcast([128, 8, 2]), op=ALU.add),
                        reads=[pf.res[b], rC], writes=[rMod])
                    pf.release(b)
                for which, (shc, scc, gi) in enumerate([(0, 1, 0), (3, 4, 2)]):
                    S.op('dve', lambda e, which=which, scc=scc, gi=gi: e.scalar_tensor_tensor(
                        out=A1[:, which, :, :], in0=modcol[:, scc, :, :], scalar=1.0,
                        in1=ngcol[:, l, gi, :].unsqueeze(2).to_broadcast([128, 8, 2]), op0=ALU.add, op1=ALU.mult),
                        reads=[rMod, rC], writes=[rMod])
                    S.op('dve', lambda e, which=which, shc=shc: e.tensor_copy(out=SH[:, which, :, :], in_=modcol[:, shc, :, :]),
                         reads=[rMod], writes=[rMod])
            S.barrier()
            with ExitStack() as ph:
                qkT = sb(ph, "qkT", [128, 14, NTOK], BF16)
                qkr = [[Res() for _ in range(NT)] for _ in range(14)]
                phv = ph.enter_context(ExitStack())
                V = sb(phv, "V", [128, NT, 640], BF16)
                Vr = [Res() for _ in range(NT)]
                with ExitStack() as ph2:
                    win = sb(ph2, "win", [128, 8, 1024], BF16)
                    rwin = Res()
                    hTr = Ring([sb(ph2, "hT%d" % i, [128, 8, 128], BF16) for i in range(2)])
                    csr = Ring([sb(ph2, "cs%d" % i, [128, 128], F32) for i in range(2)])
                    cur = {}
                    gq = sb(ph2, "gq", [128, 2, 64], F32)
                    if l == 0:
                        S.dma('sp', gq[:].rearrange("p a d -> p (a d)"), qkg_d.partition_broadcast(128), writes=[rC])
                        S.op('dve', lambda e: e.tensor_scalar_mul(out=gq[:, 0, :], in0=gq[:, 0, :], scalar1=0.125),
                             reads=[rC], writes=[rC])
                    stg = Ring([sb(ph2, "stg%d" % i, [128, 512], BF16) for i in range(3)])
                    f32t = Ring([sb(ph2, "f32t%d" % i, [128, 512], F32) for i in range(3)])

                    def transpose_store(st, rst, npt, slot0, t, src_cols=None):
                        b = pb.alloc()
                        for j in range(npt):
                            c0 = j * 128 if src_cols is None else src_cols[j]
                            S.op('pe', lambda e, j=j, c0=c0, b=b: e.transpose(pb.t[b][:, j * 128:(j + 1) * 128], st[:, c0:c0 + 128], ident[:]),
                                 reads=[rst, rC], writes=[pb.res[b]])
                        eng = evac_engine()
                        o = qkT[:, slot0:slot0 + npt, t * 128:(t + 1) * 128]
                        i_ = pb.t[b][:, 0:npt * 128].rearrange("p (s n) -> p s n", n=128)
                        wr = [qkr[s][t] for s in range(slot0, slot0 + npt)]
                        if eng == 'act':
                            S.op('act', lambda e: e.activation(out=o, in_=i_, func=AF.Copy), reads=[pb.res[b]], writes=wr)
                        else:
                            S.op('dve', lambda e: e.tensor_copy(out=o, in_=i_), reads=[pb.res[b]], writes=wr)
                        pb.release(b)

                    def rope_to(st, rst, src, src_reads, nh, t, scale, dst_cols0=0):
                        n = nh * 64
                        cs_, rcs_ = cur["cs"]
                        t1, r1 = f32t.next()
                        t2, r2 = f32t.next()
                        s4 = src.rearrange("p (h d) -> p h d", d=64)
                        S.op('dve', lambda e: e.scalar_tensor_tensor(
                            out=t1[:, 0:n].rearrange("p (h d) -> p h d", d=64), in0=s4, scalar=scale,
                            in1=cs_[:, 0:64].unsqueeze(1).to_broadcast([128, nh, 64]), op0=ALU.mult, op1=ALU.mult),
                            reads=src_reads + [rcs_], writes=[r1])
                        s5 = src.rearrange("p (h a g f) -> p h a g f", a=2, g=2, f=16)
                        o5 = t2[:, 0:n].rearrange("p (h a g f) -> p h a g f", a=2, g=2, f=16)
                        sn5 = cs_[:, 64:128].rearrange("p (a g f) -> p a g f", a=2, g=2)
                        for g in range(2):
                            for a in range(2):
                                S.op('dve', lambda e, g=g, a=a: e.scalar_tensor_tensor(
                                    out=o5[:, :, a, g, :], in0=s5[:, :, a, 1 - g, :], scalar=scale,
                                    in1=sn5[:, a, g, :].unsqueeze(1).to_broadcast([128, nh, 16]), op0=ALU.mult, op1=ALU.mult),
                                    reads=src_reads + [rcs_], writes=[r2])
                        S.op('pool', lambda e: e.tensor_tensor(out=st[:, dst_cols0:dst_cols0 + n], in0=t1[:, 0:n], in1=t2[:, 0:n], op=ALU.add),
                             reads=[r1, r2], writes=[rst])

                    def headnorm(bk, c0, nh, gi):
                        n = nh * 64
                        sq, rsq = f32t.next()
                        S.op('act', lambda e: e.activation(out=sq[:, 0:n], in_=pf.t[bk][:, c0:c0 + n], func=AF.Square),
                             reads=[pf.res[bk]], writes=[rsq])
                        ssh, rssh = scr.next()
                        S.op('dve', lambda e: e.tensor_reduce(out=ssh[:, 0:nh], in_=sq[:, 0:n].rearrange("p (h d) -> p h d", d=64), axis=AX.X, op=ALU.add),
                             reads=[rsq], writes=[rssh])
                        rs, rrs = rstd_from(ssh[:, 0:nh], 64, [rssh])
                        qn, rqn = f32t.next()
                        S.op('dve', lambda e: e.tensor_tensor(out=qn[:, 0:n].rearrange("p (h d) -> p h d", d=64),
                                                              in0=pf.t[bk][:, c0:c0 + n].rearrange("p (h d) -> p h d", d=64),
                                                              in1=rs.unsqueeze(2).to_broadcast([128, nh, 64]), op=ALU.mult),
                             reads=[pf.res[bk], rrs], writes=[rqn])
                        S.op('pool', lambda e: e.tensor_tensor(out=qn[:, 0:n].rearrange("p (h d) -> p h d", d=64),
                                                               in0=qn[:, 0:n].rearrange("p (h d) -> p h d", d=64),
                                                               in1=gq[:, gi, :].unsqueeze(1).to_broadcast([128, nh, 64]), op=ALU.mult),
                             reads=[rqn, rC], writes=[rqn])
                        return qn, rqn

                    blocks = [(0, 512), (512, 768), (768, 1280), (1280, 1792), (1792, 2304)]
                    for pblocks in ([] if os.environ.get("KDBG_SKIPPROJ") else [[0, 1], [2, 3], [4]]):
                      pc0 = blocks[pblocks[0]][0]
                      pc1 = blocks[pblocks[-1]][1]
                      for cb in range((pc1 - pc0) // 256):
                          S.dma('pool', win[:, :, cb * 256:(cb + 1) * 256], wview(win_d[l], pc0 + cb * 256, pc0 + (cb + 1) * 256), writes=[rwin])
                      need_rope = (pblocks[0] == 0) or (l == 0 and pblocks[0] == 2)
                      for t in range(NT):
                        hT, rhT = hTr.next()
                        norm_mod_transpose(t, 0, hT, rhT, 0)
                        is_x = t >= 2
                        if is_x and need_rope:
                            cs_, rcs_ = csr.next()
                            S.dma('sp', cs_[:, 0:64], cos_d[:, t - 2, :], writes=[rcs_])
                            S.dma('sp', cs_[:, 64:128], sin_d[:, t - 2, :], writes=[rcs_])
                            cur["cs"] = (cs_, rcs_)
                        for bi in pblocks:
                            c0, c1 = blocks[bi]
                            n = c1 - c0
                            bk = pf.alloc()
                            for kc in range(8):
                                S.op('pe', lambda e, kc=kc, bk=bk: e.matmul(pf.t[bk][:, 0:n], lhsT=hT[:, kc, :], rhs=win[:, kc, c0 - pc0:c1 - pc0],
                                                                        start=(kc == 0), stop=(kc == 7)),
                                     reads=[rhT, rwin], writes=[pf.res[bk]])
                            pr = pf.res[bk]
                            ps = pf.t[bk]
                            if bi == 0:
                                st, rst = stg.next()
                                if l == 0:
                                    qn, rqn = headnorm(bk, 0, 8, 0)
                                    if is_x:
                                        rope_to(st, rst, qn[:, 0:512], [rqn], 8, t, 1.0)
                                    else:
                                        S.op('act', lambda e: e.activation(out=st[:, 0:512], in_=qn[:, 0:512], func=AF.Copy), reads=[rqn], writes=[rst])
                                else:
                                    if is_x:
                                        rope_to(st, rst, ps[:, 0:512], [pr], 8, t, 0.125)
                                    else:
                                        S.op('act', lambda e: e.activation(out=st[:, 0:512], in_=ps[:, 0:512], func=AF.Identity, scale=0.125), reads=[pr], writes=[rst])
                                transpose_store(st, rst, 4, 0, t)
                            elif bi == 1:
                                st, rst = stg.next()
                                if l == 0:
                                    kn, rkn = headnorm(bk, 0, 2, 1)
                                    src, srcr = kn[:, 0:128], [rkn]
                                else:
                                    src, srcr = ps[:, 0:128], [pr]
                                if is_x:
                                    rope_to(st, rst, src, srcr, 2, t, 1.0, dst_cols0=0)
                                else:
                                    S.op('act', lambda e: e.activation(out=st[:, 0:128], in_=src, func=AF.Copy), reads=srcr, writes=[rst])
                                S.op('pool', lambda e: e.tensor_copy(
                                    out=st[:, 128:384].rearrange("p (g c d) -> p g c d", g=2, c=2),
                                    in_=st[:, 0:128].rearrange("p (g d) -> p g d", g=2).unsqueeze(2).to_broadcast([128, 2, 2, 64])),
                                    reads=[rst], writes=[rst])
                                transpose_store(st, rst, 2, 8, t, src_cols=[128, 256])
                                S.op('act', lambda e: e.activation(out=V[:, t, 0:128], in_=ps[:, 128:256], func=AF.Copy), reads=[pr], writes=[Vr[t]])
                            elif bi == 2:
                                st, rst = stg.next()
                                if l == 0 and is_x:
                                    rope_to(st, rst, ps[:, 0:512], [pr], 8, t, 0.125)
                                else:
                                    S.op('act', lambda e: e.activation(out=st[:, 0:512], in_=ps[:, 0:512], func=AF.Identity, scale=0.125), reads=[pr], writes=[rst])
                                transpose_store(st, rst, 4, 4, t)
                            elif bi == 3:
                                st, rst = stg.next()
                                if l == 0 and is_x:
                                    rope_to(st, rst, ps[:, 0:512], [pr], 8, t, 1.0)
                                else:
                                    S.op('act', lambda e: e.activation(out=st[:, 0:512], in_=ps[:, 0:512], func=AF.Copy), reads=[pr], writes=[rst])
                                transpose_store(st, rst, 4, 10, t)
                            else:
                                S.op('dve', lambda e: e.tensor_copy(out=V[:, t, 128:640], in_=ps[:, 0:512]), reads=[pr], writes=[Vr[t]])
                            pf.release(bk)
                S.barrier()
                if debug_stop == (l, 'proj'):
                    return True
                with ExitStack() as ph2:
                    Pt = Ring([sb(ph2, "Pt%d" % i, [128, 512], BF16) for i in range(16 if l == 0 else 9)])
                    f32a = Ring([sb(ph2, "f32a%d" % i, [128, 512], F32) for i in range(6 if l == 0 else 4)])
                    if l == 0:
                        lamp = sb(ph2, "lamp", [128, 4, 64], F32)
                        rl = Res()
                        S.dma('sp', lamp[:].rearrange("p a d -> p (a d)"), lamp_d.partition_broadcast(128), writes=[rl])
                        lt, rlt = scr.next()
                        for j in range(2):
                            pr_, rpr_ = f32a.next()
                            S.op('dve', lambda e, j=j, pr_=pr_: e.tensor_tensor(out=pr_[:, 0:64], in0=lamp[:, 2 * j, :], in1=lamp[:, 2 * j + 1, :], op=ALU.mult),
                                 reads=[rl], writes=[rpr_])
                            S.op('dve', lambda e, j=j, pr_=pr_: e.tensor_reduce(out=lt[:, j:j + 1], in_=pr_[:, 0:64], axis=AX.X, op=ALU.add),
                                 reads=[rpr_], writes=[rlt])
                        S.op('act', lambda e: e.activation(out=lt[:, 2:4], in_=lt[:, 0:2], func=AF.Exp), reads=[rlt], writes=[rlt])
                        S.op('dve', lambda e: e.tensor_tensor(out=lt[:, 4:5], in0=lt[:, 3:4], in1=lt[:, 2:3], op=ALU.subtract), reads=[rlt], writes=[rlt])
                        S.op('dve', lambda e: e.tensor_scalar_add(out=small[:, 0:1], in0=lt[:, 4:5], scalar1=-lambda_init(0)),
                             reads=[rlt], writes=[rC])
                        S.dma('sp', small[:, 1:2], subln_d, writes=[rC])
                        S.op('dve', lambda e: e.tensor_scalar_mul(out=small[:, 2:3], in0=small[:, 1:2], scalar1=1.0 - lambda_init(0)),
                             reads=[rC], writes=[rC])
                        neglam = small[:, 0:1]
                        sgl = small[:, 2:3]
                        qblocks = [(0, 256, [0, 1], [0, 1])] + [(256 + 512 * i, 512, list(range(NT)), [2 + 4 * i + j for j in range(4)]) for i in range(4)]
                        for (tok0, n, kts, qtiles) in qblocks:
                            for pt in range(4):
                                bo = pf.alloc()
                                bd = pf.alloc()
                                for half in range(2):
                                    h = 2 * pt + half
                                    kv = h // 4
                                    hp = slice(half * 64, (half + 1) * 64)
                                    for ki, kt in enumerate(kts):
                                        bs = pf.alloc()
                                        S.op('pe', lambda e, bs=bs, kt=kt: e.matmul(pf.t[bs][:, 0:n], lhsT=qkT[hp, 8 + kv, kt * 128:(kt + 1) * 128],
                                                                                rhs=qkT[hp, pt, tok0:tok0 + n], start=True, stop=True, tile_position=(half * 64, 0)),
                                             reads=[qkr[8 + kv][kt]] + [qkr[pt][q] for q in qtiles], writes=[pf.res[bs]])
                                        p_, rp_ = Pt.next()
                                        S.op('act', lambda e, bs=bs, p_=p_: e.activation(out=p_[:, 0:n], in_=pf.t[bs][:, 0:n], func=AF.Exp),
                                             reads=[pf.res[bs]], writes=[rp_])
                                        pf.release(bs)
                                        S.op('pe', lambda e, p_=p_, kt=kt, ki=ki: e.matmul(pf.t[bo][hp, 0:n], lhsT=V[:, kt, kv * 64:(kv + 1) * 64], rhs=p_[:, 0:n],
                                                                                       start=(ki == 0), stop=(ki == len(kts) - 1), tile_position=(0, half * 64)),
                                             reads=[Vr[kt], rp_], writes=[pf.res[bo]])
                                        S.op('pe', lambda e, p_=p_, ki=ki: e.matmul(pf.t[bd][hp, 0:n], lhsT=ones[:, 0:64], rhs=p_[:, 0:n],
                                                                                start=(ki == 0), stop=(ki == len(kts) - 1), tile_position=(0, half * 64)),
                                             reads=[rC, rp_], writes=[pf.res[bd]])
                                rr, rrr = f32a.next()
                                S.op('dve', lambda e: e.reciprocal(out=rr[:, 0:n], in_=pf.t[bd][:, 0:n]), reads=[pf.res[bd]], writes=[rrr])
                                S.op('dve', lambda e: e.tensor_tensor(out=qkT[:, pt, tok0:tok0 + n], in0=pf.t[bo][:, 0:n], in1=rr[:, 0:n], op=ALU.mult),
                                     reads=[pf.res[bo], rrr], writes=[qkr[pt][q] for q in qtiles])
                                pf.release(bo)
                                pf.release(bd)
                            for hb in range(4):
                                bo = [pf.alloc(), pf.alloc()]
                                bd = [pf.alloc(), pf.alloc()]
                                for j in range(2):
                                    hp = slice(j * 64, (j + 1) * 64)
                                    for ki, kt in enumerate(kts):
                                        bs = pf.alloc()
                                        S.op('pe', lambda e, bs=bs, kt=kt: e.matmul(pf.t[bs][:, 0:n], lhsT=qkT[hp, 10 + hb, kt * 128:(kt + 1) * 128],
                                                                                rhs=qkT[hp, 4 + hb, tok0:tok0 + n], start=True, stop=True, tile_position=(j * 64, 0)),
                                             reads=[qkr[10 + hb][kt]] + [qkr[4 + hb][q] for q in qtiles], writes=[pf.res[bs]])
                                        p_, rp_ = Pt.next()
                                        S.op('act', lambda e, bs=bs, p_=p_: e.activation(out=p_[:, 0:n], in_=pf.t[bs][:, 0:n], func=AF.Exp),
                                             reads=[pf.res[bs]], writes=[rp_])
                                        pf.release(bs)
                                        S.op('pe', lambda e, p_=p_, kt=kt, ki=ki, j=j: e.matmul(pf.t[bo[j]][:, 0:n], lhsT=V[:, kt, 128 + hb * 128:128 + (hb + 1) * 128], rhs=p_[:, 0:n],
                                                                                            start=(ki == 0), stop=(ki == len(kts) - 1)),
                                             reads=[Vr[kt], rp_], writes=[pf.res[bo[j]]])
                                        S.op('pe', lambda e, p_=p_, ki=ki, j=j: e.matmul(pf.t[bd[j]][:, 0:n], lhsT=ones[:, :], rhs=p_[:, 0:n],
                                                                                     start=(ki == 0), stop=(ki == len(kts) - 1)),
                                             reads=[rC, rp_], writes=[pf.res[bd[j]]])
                                oo = []
                                for j in range(2):
                                    rr, rrr = f32a.next()
                                    S.op('dve', lambda e, j=j, rr=rr: e.reciprocal(out=rr[:, 0:n], in_=pf.t[bd[j]][:, 0:n]), reads=[pf.res[bd[j]]], writes=[rrr])
                                    o_, ro_ = f32a.next()
                                    S.op('dve', lambda e, j=j, rr=rr, o_=o_: e.tensor_tensor(out=o_[:, 0:n], in0=pf.t[bo[j]][:, 0:n], in1=rr[:, 0:n], op=ALU.mult),
                                         reads=[pf.res[bo[j]], rrr], writes=[ro_])
                                    oo.append((o_, ro_))
                                for j in range(2):
                                    pf.release(bo[j])
                                    pf.release(bd[j])
                                ob, rob = f32a.next()
                                S.op('dve', lambda e: e.scalar_tensor_tensor(out=ob[:, 0:n], in0=oo[1][0][:, 0:n], scalar=neglam, in1=oo[0][0][:, 0:n],
                                                                              op0=ALU.mult, op1=ALU.add),
                                     reads=[oo[0][1], oo[1][1], rC], writes=[rob])
                                sq, rsq = Pt.next()
                                S.op('pool', lambda e: e.tensor_tensor(out=sq[:, 0:n], in0=ob[:, 0:n], in1=ob[:, 0:n], op=ALU.mult), reads=[rob], writes=[rsq])
                                bq = pf.alloc()
                                S.op('pe', lambda e: e.matmul(pf.t[bq][:, 0:n], lhsT=ones[:, :], rhs=sq[:, 0:n], start=True, stop=True),
                                     reads=[rC, rsq], writes=[pf.res[bq]])
                                rs_, rrs_ = f32a.next()
                                S.op('dve', lambda e: e.tensor_scalar(out=rs_[:, 0:n], in0=pf.t[bq][:, 0:n], scalar1=1.0 / 128, scalar2=EPS, op0=ALU.mult, op1=ALU.add),
                                     reads=[pf.res[bq]], writes=[rrs_])
                                pf.release(bq)
                                S.op('act', lambda e: e.activation(out=rs_[:, 0:n], in_=rs_[:, 0:n], func=AF.Sqrt), reads=[rrs_], writes=[rrs_])
                                S.op('dve', lambda e: e.reciprocal(out=rs_[:, 0:n], in_=rs_[:, 0:n]), reads=[rrs_], writes=[rrs_])
                                S.op('dve', lambda e: e.scalar_tensor_tensor(out=qkT[:, 4 + hb, tok0:tok0 + n], in0=ob[:, 0:n], scalar=sgl, in1=rs_[:, 0:n],
                                                                             op0=ALU.mult, op1=ALU.mult),
                                     reads=[rob, rrs_, rC], writes=[qkr[4 + hb][q] for q in qtiles])
                    else:
                        tbl = sb(ph2, "tbl", [128, 4, 16, 64], F32)
                        rtbl = Res()
                        qzr = Ring([sb(ph2, "qz%d" % i, [128, 2, 128], BF16) for i in range(4)])
                        for qz_ in qzr.t:
                            S.op('pool', lambda e, qz_=qz_: e.memset(qz_[:], 0.0), writes=[rC])

                        def make_qz(slot, t):
                            qz_, rqz_ = qzr.next()
                            S.op('pool', lambda e: e.tensor_copy(out=qz_[0:64, 0, :], in_=qkT[0:64, slot, t * 128:(t + 1) * 128]),
                                 reads=[qkr[slot][t], rC], writes=[rqz_])
                            S.op('dve', lambda e: e.tensor_copy(out=qz_[64:128, 1, :], in_=qkT[64:128, slot, t * 128:(t + 1) * 128]),
                                 reads=[qkr[slot][t], rC], writes=[rqz_])
                            return qz_, rqz_
                        mprev = sb(ph2, "mprev", [128, 128], BF16)
                        mnext = sb(ph2, "mnext", [128, 128], BF16)
                        S.dma('pool', mprev[:], mprev_d, writes=[rC])
                        S.dma('pool', mnext[:], mnext_d, writes=[rC])
                        esink = sb(ph2, "esink", [128, 8], F32)
                        S.dma('sp', esink[:], sink_d.partition_broadcast(128), writes=[rC])
                        S.op('act', lambda e: e.activation(out=esink[:], in_=esink[:], func=AF.Exp), reads=[rC], writes=[rC])
                        for i in range(int(os.environ.get("KDBG_NI", "16"))):
                            t = i + 2
                            tok0 = t * 128
                            for g in range(2):
                                klist = [(0, None), (1, None)]
                                if i >= 1:
                                    klist.append((t - 1, mprev))
                                klist.append((t, None))
                                if i <= 14:
                                    klist.append((t + 1, mnext))
                                pts = []
                                qzs = [make_qz(2 * g + j, t) for j in range(2)]
                                for (kt, msk) in klist:
                                    bs = pf.alloc()
                                    for hq in range(4):
                                        half = hq % 2
                                        qz_, rqz_ = qzs[hq // 2]
                                        S.op('pe', lambda e, bs=bs, kt=kt, hq=hq, half=half, qz_=qz_: e.matmul(
                                            pf.t[bs][:, hq * 128:(hq + 1) * 128], lhsT=qkT[:, 8 + g, kt * 128:(kt + 1) * 128],
                                            rhs=qz_[:, half, :], start=True, stop=True),
                                            reads=[qkr[8 + g][kt], rqz_], writes=[pf.res[bs]])
                                    p_, rp_ = Pt.next()
                                    S.op('act', lambda e, bs=bs, p_=p_: e.activation(out=p_[:], in_=pf.t[bs][:], func=AF.Exp), reads=[pf.res[bs]], writes=[rp_])
                                    pf.release(bs)
                                    if msk is not None:
                                        S.op('pool', lambda e, p_=p_, msk=msk: e.tensor_tensor(
                                            out=p_[:].rearrange("p (h q) -> p h q", h=4), in0=p_[:].rearrange("p (h q) -> p h q", h=4),
                                            in1=msk[:].unsqueeze(1).to_broadcast([128, 4, 128]), op=ALU.mult), reads=[rp_, rC], writes=[rp_])
                                    pts.append((kt, p_, rp_))
                                bo = pf.alloc()
                                bd = pf.alloc()
                                for hq in range(4):
                                    ptl = hq // 2
                                    half = hq % 2
                                    hp = slice(half * 64, (half + 1) * 64)
                                    for ki, (kt, p_, rp_) in enumerate(pts):
                                        S.op('pe', lambda e, kt=kt, p_=p_, ki=ki, hq=hq, ptl=ptl, hp=hp, half=half: e.matmul(
                                            pf.t[bo][hp, ptl * 128:(ptl + 1) * 128], lhsT=V[:, kt, g * 64:(g + 1) * 64], rhs=p_[:, hq * 128:(hq + 1) * 128],
                                            start=(ki == 0), stop=(ki == len(pts) - 1), tile_position=(0, half * 64)),
                                            reads=[Vr[kt], rp_], writes=[pf.res[bo]])
                                        S.op('pe', lambda e, p_=p_, ki=ki, hq=hq, ptl=ptl, hp=hp, half=half: e.matmul(
                                            pf.t[bd][hp, ptl * 128:(ptl + 1) * 128], lhsT=ones[:, 0:64], rhs=p_[:, hq * 128:(hq + 1) * 128],
                                            start=(ki == 0), stop=(ki == len(pts) - 1), tile_position=(0, half * 64)),
                                            reads=[rC, rp_], writes=[pf.res[bd]])
                                rr, rrr = f32a.next()
                                for hq in range(4):
                                    ptl = hq // 2
                                    half = hq % 2
                                    hp = slice(half * 64, (half + 1) * 64)
                                    h = 4 * g + hq
                                    S.op('dve', lambda e, hp=hp, ptl=ptl, h=h: e.tensor_scalar_add(
                                        out=rr[hp, ptl * 128:(ptl + 1) * 128], in0=pf.t[bd][hp, ptl * 128:(ptl + 1) * 128],
                                        scalar1=esink[hp, h:h + 1]), reads=[pf.res[bd], rC], writes=[rrr])
                                S.op('dve', lambda e: e.reciprocal(out=rr[:, 0:256], in_=rr[:, 0:256]), reads=[rrr], writes=[rrr])
                                S.op('dve', lambda e: e.tensor_tensor(out=qkT[:, 2 * g:2 * g + 2, tok0:tok0 + 128],
                                                                      in0=pf.t[bo][:, 0:256].rearrange("p (s q) -> p s q", s=2),
                                                                      in1=rr[:, 0:256].rearrange("p (s q) -> p s q", s=2), op=ALU.mult),
                                     reads=[pf.res[bo], rrr], writes=[qkr[2 * g][t], qkr[2 * g + 1][t]])
                                pf.release(bo)
                                pf.release(bd)
                        if debug_stop == (l, 'c'):
                            return True
                        for hg in range(2):
                            S.dma('sp', tbl[:], tbl_d[:, hg * 4:(hg + 1) * 4], writes=[rtbl])
                            for i in range(16):
                                t = i + 2
                                tok0 = t * 128
                                r0, r1 = 2 * i, 2 * i + 1
                                rs0 = min(max(r0 - 4, 0), 24)
                                rs1 = min(max(r1 - 4, 0), 24)
                                kx_lo = rs0 // 2
                                kx_hi = (rs1 + 8 + 1) // 2
                                klist = [(0, None), (1, None)]
                                for kx in range(kx_lo, kx_hi):
                                    jj = {}
                                    for kr in range(2):
                                        for qr in range(2):
                                            krow = 2 * kx + kr
                                            qrow = 2 * i + qr
                                            rs_q = min(max(qrow - 4, 0), 24)
                                            if rs_q <= krow < rs_q + 8:
                                                jj[(kr, qr)] = krow - qrow + 7
                                            else:
                                                jj[(kr, qr)] = 15
                                    klist.append((kx + 2, jj))
                                pts = []
                                qzs = [make_qz(4 + hg * 2 + j, t) for j in range(2)]
                                for (kt, jj) in klist:
                                    bs = pf.alloc()
                                    for hq in range(4):
                                        h = hg * 4 + hq
                                        half = h % 2
                                        qz_, rqz_ = qzs[hq // 2]
                                        S.op('pe', lambda e, bs=bs, kt=kt, hq=hq, h=h, half=half, qz_=qz_: e.matmul(
                                            pf.t[bs][:, hq * 128:(hq + 1) * 128], lhsT=qkT[:, 10 + h // 2, kt * 128:(kt + 1) * 128],
                                            rhs=qz_[:, half, :], start=True, stop=True),
                                            reads=[qkr[10 + h // 2][kt], rqz_], writes=[pf.res[bs]])
                                    p_, rp_ = Pt.next()
                                    if jj is None:
                                        S.op('act', lambda e, bs=bs, p_=p_: e.activation(out=p_[:], in_=pf.t[bs][:], func=AF.Exp), reads=[pf.res[bs]], writes=[rp_])
                                    else:
                                        sbf, rsbf = f32a.next()
                                        for kr in range(2):
                                            for qr in range(2):
                                                kp = slice(kr * 64, (kr + 1) * 64)
                                                S.op('dve', lambda e, bs=bs, kp=kp, qr=qr, kr=kr, jj=jj, sbf=sbf: e.tensor_tensor(
                                                    out=sbf[kp, :].rearrange("p (h q) -> p h q", h=4)[:, :, qr * 64:(qr + 1) * 64],
                                                    in0=pf.t[bs][kp, :].rearrange("p (h q) -> p h q", h=4)[:, :, qr * 64:(qr + 1) * 64],
                                                    in1=tbl[kp, :, jj[(kr, qr)], :], op=ALU.add),
                                                    reads=[pf.res[bs], rtbl], writes=[rsbf])
                                        S.op('act', lambda e, sbf=sbf, p_=p_: e.activation(out=p_[:], in_=sbf[:], func=AF.Exp), reads=[rsbf], writes=[rp_])
                                    pf.release(bs)
                                    pts.append((kt, p_, rp_))
                                bo = pf.alloc()
                                bd = pf.alloc()
                                for hq in range(4):
                                    h = hg * 4 + hq
                                    ptl = hq // 2
                                    half = hq % 2
                                    hp = slice(half * 64, (half + 1) * 64)
                                    for ki, (kt, p_, rp_) in enumerate(pts):
                                        S.op('pe', lambda e, kt=kt, p_=p_, ki=ki, hq=hq, ptl=ptl, hp=hp, half=half, h=h: e.matmul(
                                            pf.t[bo][hp, ptl * 128:(ptl + 1) * 128], lhsT=V[:, kt, 128 + h * 64:128 + (h + 1) * 64], rhs=p_[:, hq * 128:(hq + 1) * 128],
                                            start=(ki == 0), stop=(ki == len(pts) - 1), tile_position=(0, half * 64)),
                                            reads=[Vr[kt], rp_], writes=[pf.res[bo]])
                                        S.op('pe', lambda e, p_=p_, ki=ki, hq=hq, ptl=ptl, hp=hp, half=half: e.matmul(
                                            pf.t[bd][hp, ptl * 128:(ptl + 1) * 128], lhsT=ones[:, 0:64], rhs=p_[:, hq * 128:(hq + 1) * 128],
                                            start=(ki == 0), stop=(ki == len(pts) - 1), tile_position=(0, half * 64)),
                                            reads=[rC, rp_], writes=[pf.res[bd]])
                                rr, rrr = f32a.next()
                                S.op('dve', lambda e: e.reciprocal(out=rr[:, 0:256], in_=pf.t[bd][:, 0:256]), reads=[pf.res[bd]], writes=[rrr])
                                s0 = 4 + hg * 2
                                S.op('dve', lambda e: e.tensor_tensor(out=qkT[:, s0:s0 + 2, tok0:tok0 + 128],
                                                                      in0=pf.t[bo][:, 0:256].rearrange("p (s q) -> p s q", s=2),
                                                                      in1=rr[:, 0:256].rearrange("p (s q) -> p s q", s=2), op=ALU.mult),
                                     reads=[pf.res[bo], rrr], writes=[qkr[s0][t], qkr[s0 + 1][t]])
                                pf.release(bo)
                                pf.release(bd)
                S.barrier()
                phv.close()
                with ExitStack() as ph2:
                    wo = sb(ph2, "wo", [128, 8, D], BF16)
                    rwo = Res()
                    G = [sb(ph2, "G%d" % i, [128, D], F32) for i in which_list]
                    rG = [Res() for _ in which_list]
                    Gtmp = sb(ph2, "Gtmp", [128, D], F32)
                    rGtmp = Res()
                    gate_rows(l, 2, 1, wo, rwo, G, rG, which_list, Gtmp, rGtmp)
                    for h in range(2):
                        S.dma('pool', wo[:, :, h * 512:(h + 1) * 512], wview(wout_d[l], h * 512, (h + 1) * 512), writes=[rwo])
                    for t in tiles_upd:
                        ci = 1 if t < 2 else 0
                        banks = [pf.alloc(), pf.alloc()]
                        for c in range(2):
                            for s in range(8):
                                S.op('pe', lambda e, c=c, s=s: e.matmul(pf.t[banks[c]][:], lhsT=qkT[:, s, t * 128:(t + 1) * 128],
                                                                      rhs=wo[:, s, c * 512:(c + 1) * 512], start=(s == 0), stop=(s == 7)),
                                     reads=[qkr[s][t], rwo], writes=[pf.res[banks[c]]])
                        residual_epilogue(t, banks, G[ci], rG[ci])
                        pf.release(banks[0])
                        pf.release(banks[1])
            S.barrier()
            if debug_stop == (l, 'attn'):
                return True
            with ExitStack() as ph:
                W2 = sb(ph, "W2", [128, 32, D], BF16)
                rW2 = Res()
                G = [sb(ph, "G2_%d" % i, [128, D], F32) for i in which_list]
                rG = [Res() for _ in which_list]
                hidT = sb(ph, "hidT", [128, 32, 384], BF16)
                rhid = Res()
                with ExitStack() as phg:
                    Gtmp = sb(phg, "Gtmp2", [128, D], F32)
                    rGtmp = Res()
                    gate_rows(l, 5, 3, W2[:, 0:8, :], rW2, G, rG, which_list, Gtmp, rGtmp)
                    S.barrier()
                w2v = w2_d[l].rearrange("(hc p) n -> p hc n", p=128)
                for q in range(8):
                    S.dma('pool', W2[:, q * 4:(q + 1) * 4, :], w2v[:, q * 4:(q + 1) * 4, :], writes=[rW2])
                hT2r = Ring([sb(ph, "hT2_%d" % i, [128, 8, 384], BF16) for i in range(1)])
                w1r = Ring([sb(ph, "w1_%d" % i, [128, 8, 256], BF16) for i in range(2)])
                relu_t = Ring([sb(ph, "relu%d" % i, [128, 384], F32) for i in range(2)])
                blocks = [tiles_upd[i:i + 3] for i in range(0, len(tiles_upd), 3)]
                for blk in blocks:
                    nb = len(blk) * 128
                    hT2, rh2 = hT2r.next()
                    for j, t in enumerate(blk):
                        norm_mod_transpose(t, 1, hT2, rh2, j * 128)
                    for p in range(16):
                        w1, rw1 = w1r.next()
                        S.dma('pool', w1[:], wview(w1_d[l], p * 256, (p + 1) * 256), writes=[rw1])
                        for h2 in range(2):
                            hc = p * 2 + h2
                            bk = pf.alloc()
                            for kc in range(8):
                                S.op('pe', lambda e, kc=kc, bk=bk, h2=h2, w1=w1: e.matmul(pf.t[bk][:, 0:nb], lhsT=w1[:, kc, h2 * 128:(h2 + 1) * 128],
                                                                                   rhs=hT2[:, kc, 0:nb], start=(kc == 0), stop=(kc == 7)),
                                     reads=[rw1, rh2], writes=[pf.res[bk]])
                            rl_, rrl_ = relu_t.next()
                            S.op('act', lambda e, bk=bk, rl_=rl_: e.activation(out=rl_[:, 0:nb], in_=pf.t[bk][:, 0:nb], func=AF.Relu),
                                 reads=[pf.res[bk]], writes=[rrl_])
                            pf.release(bk)
                            S.op('pool', lambda e, hc=hc, rl_=rl_: e.tensor_tensor(out=hidT[:, hc, 0:nb], in0=rl_[:, 0:nb], in1=rl_[:, 0:nb], op=ALU.mult),
                                 reads=[rrl_], writes=[rhid])
                    for j, t in enumerate(blk):
                        ci = 1 if t < 2 else 0
                        banks = [pf.alloc(), pf.alloc()]
                        for c in range(2):
                            for hc in range(32):
                                S.op('pe', lambda e, c=c, hc=hc, j=j: e.matmul(pf.t[banks[c]][:], lhsT=hidT[:, hc, j * 128:(j + 1) * 128],
                                                                            rhs=W2[:, hc, c * 512:(c + 1) * 512], start=(hc == 0), stop=(hc == 31)),
                                     reads=[rhid, rW2], writes=[pf.res[banks[c]]])
                        residual_epilogue(t, banks, G[ci], rG[ci])
                        pf.release(banks[0])
                        pf.release(banks[1])
            if debug_stop == (l, 'mlp'):
                return True

        for l in layers:
            if run_layer(l):
                break
        S.barrier()
        yv = y_d.rearrange("(t p) d -> p t d", p=128)
        ro = Res()
        if out_all:
            S.dma('sp', yv[:, 0:2, :], X[:, 0:2, :], reads=Xr[0:2], writes=[ro])
        yo = 2 if out_all else 0
        for i in range(4):
            S.dma('sp', yv[:, yo + 4 * i:yo + 4 * i + 4, :], X[:, 2 + 4 * i:6 + 4 * i, :], reads=Xr[2 + 4 * i:6 + 4 * i], writes=[ro])
        S.barrier()
    return nc


def _host_constants():
    ident = np.eye(128, dtype=np.float32)
    t = np.arange(T)
    pos = np.stack([t // 64, t % 64], -1).astype(np.float32)
    inv = (np.float32(10000.0) ** (-np.arange(16, dtype=np.float32) / np.float32(16))).astype(np.float32)
    ang = pos[..., None] * inv
    cos = np.cos(ang).astype(np.float32)
    sin = np.sin(ang).astype(np.float32)
    C = np.stack([cos, cos], axis=2)
    Sg = np.stack([-sin, sin], axis=2)
    cos_t = np.ascontiguousarray(C.reshape(16, 128, 64).transpose(1, 0, 2))
    sin_t = np.ascontiguousarray(Sg.reshape(16, 128, 64).transpose(1, 0, 2))
    a = np.arange(128)[:, None]
    b = np.arange(128)[None, :]
    mprev = (b <= a).astype(np.float32)
    mnext = (a <= b).astype(np.float32)
    return dict(ident=ident, cos_t=cos_t, sin_t=sin_t, mprev=mprev, mnext=mnext)


def _tbl_from_rpb(rpb):
    kc = (np.arange(128) % 64)[:, None]
    qc = np.arange(64)[None, :]
    cs = np.clip(qc - 8, 0, 48)
    valid = (kc >= cs) & (kc < cs + 16)
    dc = np.clip(kc - qc, -15, 15) + 15
    ext = np.concatenate([rpb.reshape(8, -1), np.full((8, 1), NEGV, np.float32)], axis=1)
    idx = np.empty((128, 16, 64), np.int64)
    for j in range(15):
        idx[:, j, :] = np.where(valid, j * 31 + dc, 465)
    idx[:, 15, :] = 465
    tbl = ext[:, idx]
    return np.ascontiguousarray(tbl.transpose(1, 0, 2, 3)).astype(np.float32)


_CACHE = {}
_MODE = "fused"


def kernel(x, c, ctx, c_ctx, w_mod, b_mod, norm_g, w_in, w_out, w_mlp_in, w_mlp_out,
           qk_norm_a, diff_lambda, diff_subln, sink_c, rpb_d, _debug_stop=None):
    f = lambda a: np.ascontiguousarray(np.asarray(a, dtype=np.float32))
    x, c, ctx, c_ctx = f(x), f(c), f(ctx), f(c_ctx)
    w_mod, b_mod, norm_g, w_in, w_out, w_mlp_in, w_mlp_out = map(f, (w_mod, b_mod, norm_g, w_in, w_out, w_mlp_in, w_mlp_out))
    qk_norm_a, diff_lambda, diff_subln, sink_c, rpb_d = map(f, (qk_norm_a, diff_lambda, diff_subln, sink_c, rpb_d))
    def get_nc(layers):
        key = ("nc", layers, _debug_stop)
        if key not in _CACHE:
            _CACHE[key] = build_program(layers, _debug_stop)
        return _CACHE[key]
    consts = _host_constants()
    shared = dict(
        w_mod=w_mod, b_mod=b_mod, norm_g=norm_g, w_in=w_in, w_out=w_out, w_mlp_in=w_mlp_in, w_mlp_out=w_mlp_out,
        bmod_col=f(b_mod.reshape(2, 48, 128).transpose(2, 0, 1)),
        ng_col=f(norm_g.reshape(2, 4, 8, 128).transpose(3, 0, 1, 2)),
        qkg=f(qk_norm_a[0].reshape(1, 128)), lamp=f(diff_lambda[0].reshape(1, 256)),
        subln_col=f(diff_subln[0].reshape(128, 1)), sink=f(sink_c[0].reshape(1, 8)),
        tbl=_tbl_from_rpb(rpb_d[0]), **consts)
    def launch(layers, xs, cs):
        in_maps = []
        for b in range(8):
            m = dict(shared)
            m["x"] = f(xs[b])
            m["ctx"] = f(cs[b])
            m["ccol"] = f(np.stack([c[b].reshape(8, 128).T, c_ctx.reshape(8, 128).T], -1))
            in_maps.append(m)
        res = run_bass_kernel_spmd(get_nc(layers), in_maps, core_ids=list(range(8)))
        return [r["y"] for r in res.results]

    if _debug_stop is not None:
        ys = launch((_debug_stop[0],), x, ctx)
        return np.stack(ys, 0).astype(np.float32)
    if _MODE == "fused":
        ys = launch((0, 1), x, ctx)
        return np.stack(ys, 0).astype(np.float32)
    ys = launch((0,), x, ctx)
    ys = launch((1,), [y[LC:] for y in ys], [y[:LC] for y in ys])
    return np.stack(ys, 0).astype(np.float32)
```

```python
import numpy as np
import math
import os
from contextlib import ExitStack
import concourse.bass as bass
import concourse.mybir as mybir
from concourse.bass_utils import run_bass_kernel_spmd

F32 = mybir.dt.float32
BF16 = mybir.dt.bfloat16
ALU = mybir.AluOpType
AF = mybir.ActivationFunctionType
AX = mybir.AxisListType

D = 1024
T = 2048
LC = 256
NT = 18
NTOK = NT * 128
EPS = 1e-6
NEGV = -30000.0
NLAYERS = 2


class _Stop(Exception):
    pass


class Res:
    __slots__ = ("w", "r")

    def __init__(self):
        self.w = None
        self.r = {}


class Sched:
    NDMA = 24

    def __init__(self, nc, es):
        self.nc = nc
        self.eng = dict(pe=nc.tensor, dve=nc.vector, act=nc.scalar, pool=nc.gpsimd, sp=nc.sync)
        self.esem = {k: es.enter_context(nc.semaphore("s_" + k)) for k in self.eng}
        self.ecnt = {k: 0 for k in self.eng}
        self.known = {k: {} for k in self.eng}
        self.dsem = [es.enter_context(nc.semaphore("d%d" % i)) for i in range(self.NDMA)]
        self.dcnt = [0] * self.NDMA
        half = self.NDMA // 2
        self.dq = {'pool': list(range(0, half)), 'sp': list(range(half, self.NDMA))}
        self.dnext = {'pool': 0, 'sp': 0}

    def _wait(self, e, tok):
        key, val = tok
        if key == 'pe' and e == 'pe':
            return
        if self.known[e].get(key, 0) >= val:
            return
        sem = self.esem[key] if isinstance(key, str) else self.dsem[key]
        self.eng[e].wait_ge(sem, val)
        self.known[e][key] = val

    def _deps(self, e, reads, writes):
        for r in reads:
            if r.w is not None:
                self._wait(e, r.w)
        for w in writes:
            if w.w is not None:
                self._wait(e, w.w)
            for k, v in w.r.items():
                self._wait(e, (k, v))

    def _commit(self, tok, reads, writes):
        k, v = tok
        for r in reads:
            if r.r.get(k, 0) < v:
                r.r[k] = v
        for w in writes:
            w.w = tok
            w.r = {}

    def op(self, e, fn, reads=(), writes=()):
        self._deps(e, reads, writes)
        ins = fn(self.eng[e])
        self.ecnt[e] += 1
        ins.then_inc(self.esem[e], 1)
        self._commit((e, self.ecnt[e]), reads, writes)
        return ins

    def dma(self, e, out, in_, reads=(), writes=()):
        self._deps(e, reads, writes)
        q = self.dq[e]
        i = q[self.dnext[e]]
        self.dnext[e] = (self.dnext[e] + 1) % len(q)
        if self.dcnt[i] > 0:
            self._wait(e, (i, self.dcnt[i]))
        ins = self.eng[e].dma_start(out=out, in_=in_)
        self.dcnt[i] += 16
        ins.then_inc(self.dsem[i], 16)
        self._commit((i, self.dcnt[i]), reads, writes)
        return ins

    def barrier(self):
        for e in self.eng:
            for k in self.eng:
                if k != e and self.ecnt[k] > 0:
                    self._wait(e, (k, self.ecnt[k]))
            for i in range(self.NDMA):
                if self.dcnt[i] > 0:
                    self._wait(e, (i, self.dcnt[i]))


class Banks:
    def __init__(self, tensors):
        self.t = tensors
        self.res = [Res() for _ in tensors]
        self.free = list(range(len(tensors)))

    def alloc(self):
        i = self.free.pop(0)
        return i

    def release(self, i):
        self.free.append(i)


class Ring:
    def __init__(self, tiles):
        self.t = tiles
        self.res = [Res() for _ in tiles]
        self.i = 0

    def next(self):
        i = self.i
        self.i = (i + 1) % len(self.t)
        return self.t[i], self.res[i]


def lambda_init(layer):
    return 0.8 - 0.6 * math.exp(-0.3 * layer)


def build_program(layers=(0, 1), debug_stop=None):
    nc = bass.Bass("TRN2", target_bir_lowering=False)
    dt_in = lambda name, shape: nc.dram_tensor(name, shape, F32, kind="ExternalInput").ap()
    x_d = dt_in("x", [T, D])
    ctx_d = dt_in("ctx", [LC, D])
    ccol_d = dt_in("ccol", [128, 8, 2])
    wmod_d = dt_in("w_mod", [2, D, 6 * D])
    bmodcol_d = dt_in("bmod_col", [128, 2, 48])
    bmod_d = dt_in("b_mod", [2, 6 * D])
    ngcol_d = dt_in("ng_col", [128, 2, 4, 8])
    ng_d = dt_in("norm_g", [2, 4, D])
    win_d = dt_in("w_in", [2, D, 2304])
    wout_d = dt_in("w_out", [2, D, D])
    w1_d = dt_in("w_mlp_in", [2, D, 4 * D])
    w2_d = dt_in("w_mlp_out", [2, 4 * D, D])
    qkg_d = dt_in("qkg", [1, 128])
    lamp_d = dt_in("lamp", [1, 256])
    subln_d = dt_in("subln_col", [128, 1])
    sink_d = dt_in("sink", [1, 8])
    tbl_d = dt_in("tbl", [128, 8, 16, 64])
    ident_d = dt_in("ident", [128, 128])
    cos_d = dt_in("cos_t", [128, 16, 64])
    sin_d = dt_in("sin_t", [128, 16, 64])
    mprev_d = dt_in("mprev", [128, 128])
    mnext_d = dt_in("mnext", [128, 128])
    out_all = (layers[-1] != NLAYERS - 1)
    y_d = nc.dram_tensor("y", [NTOK if out_all else T, D], F32, kind="ExternalOutput").ap()

    with ExitStack() as es:
        S = Sched(nc, es)

        uid = [0]

        def sb(scope, name, shape, dt):
            uid[0] += 1
            return scope.enter_context(nc.sbuf_tensor("sb%d_%s" % (uid[0], name), shape, dt))

        X = sb(es, "X", [128, NT, D], F32)
        Xr = [Res() for _ in range(NT)]
        ident = sb(es, "ident", [128, 128], BF16)
        ones = sb(es, "ones", [128, 128], BF16)
        sT = sb(es, "sT", [128, 8, 2], BF16)
        sRep = sb(es, "sRep", [128, 2, 8, 128], BF16)
        small = sb(es, "small", [128, 64], F32)
        modcol = sb(es, "modcol", [128, 6, 8, 2], F32)
        A1 = sb(es, "A1", [128, 2, 8, 2], F32)
        SH = sb(es, "SH", [128, 2, 8, 2], F32)
        bmodcol = sb(es, "bmodcol", [128, 2, 48], F32)
        ngcol = sb(es, "ngcol", [128, 2, 4, 8], F32)
        rC = Res()
        rMod = Res()
        sc_t = [sb(es, "sc%d" % i, [128, 16], F32) for i in range(6)]
        scr = Ring(sc_t)
        ybf = Ring([sb(es, "ybf%d" % i, [128, D], BF16) for i in range(2)])
        rtmp = Ring([sb(es, "rtmp%d" % i, [128, 512], F32) for i in range(2)])
        pf = Banks([es.enter_context(nc.psum_tensor("pf%d" % i, [128, 512], F32)) for i in range(6)])
        pb = Banks([es.enter_context(nc.psum_tensor("pb%d" % i, [128, 1024], BF16)) for i in range(2)])

        S.dma('pool', ident[:], ident_d, writes=[rC])
        S.op('dve', lambda e: e.memset(ones[:], 1.0), writes=[rC])
        S.dma('sp', bmodcol[:], bmodcol_d, writes=[rC])
        S.dma('sp', ngcol[:], ngcol_d, writes=[rC])
        cc, rcc = scr.next()
        S.dma('sp', cc[:, 0:16].rearrange("p (k i) -> p k i", i=2), ccol_d, writes=[rcc])
        sf, rsf = scr.next()
        S.op('act', lambda e: e.activation(out=sf[:, 0:16], in_=cc[:, 0:16], func=AF.Silu), reads=[rcc], writes=[rsf])
        S.op('dve', lambda e: e.tensor_copy(out=sT[:], in_=sf[:, 0:16].rearrange("p (k i) -> p k i", i=2)), reads=[rsf], writes=[rC])
        for i in range(2):
            S.op('dve', lambda e, i=i: e.tensor_copy(
                out=sRep[:, i, :, :],
                in_=sf[:, 0:16].rearrange("p (k i) -> p k i", i=2)[:, :, i:i + 1].to_broadcast([128, 8, 128])),
                reads=[rsf], writes=[rC])
        S.dma('sp', X[:, 0:2, :], ctx_d.rearrange("(t p) d -> p t d", p=128), writes=Xr[0:2])
        xv = x_d.rearrange("(t p) d -> p t d", p=128)
        for i in range(4):
            S.dma('sp', X[:, 2 + 4 * i:6 + 4 * i, :], xv[:, 4 * i:4 * i + 4, :], writes=Xr[2 + 4 * i:6 + 4 * i])

        def wview(ap2d, c0, c1):
            return ap2d[:, c0:c1].rearrange("(kc p) n -> p kc n", p=128)

        def rstd_from(ss_ap, n, reads):
            m = ss_ap.shape[1]
            t, r = scr.next()
            S.op('dve', lambda e: e.tensor_scalar(out=t[:, 0:m], in0=ss_ap, scalar1=1.0 / n, scalar2=EPS,
                                                  op0=ALU.mult, op1=ALU.add), reads=reads, writes=[r])
            S.op('act', lambda e: e.activation(out=t[:, 0:m], in_=t[:, 0:m], func=AF.Sqrt), reads=[r], writes=[r])
            S.op('dve', lambda e: e.reciprocal(out=t[:, 0:m], in_=t[:, 0:m]), reads=[r], writes=[r])
            return t[:, 0:m], r

        def pipeline(kts, qk_fn, pv_fn, depth=1):
            pend = []
            for ki, kt in enumerate(kts):
                p_, rp_ = qk_fn(kt)
                pend.append((ki, kt, p_, rp_))
                if len(pend) > depth:
                    pv_fn(*pend.pop(0))
            while pend:
                pv_fn(*pend.pop(0))

        cnt = {"ev": 0}

        def evac_engine():
            cnt["ev"] += 1
            return 'act' if cnt["ev"] % 2 == 0 else 'dve'

        def norm_mod_transpose(t, which, dst, dst_res, dst_off):
            ci = 1 if t < 2 else 0
            ss, rss = scr.next()
            y, ry = ybf.next()
            S.op('act', lambda e: e.activation(out=y[:], in_=X[:, t, :], func=AF.Square, accum_out=ss[:, 0:1]),
                 reads=[Xr[t]], writes=[ry, rss])
            rs, rrs = rstd_from(ss[:, 0:1], D, [rss])
            S.op('dve', lambda e: e.tensor_scalar_mul(out=y[:], in0=X[:, t, :], scalar1=rs[:, 0:1]), reads=[Xr[t], rrs], writes=[ry])
            b = pb.alloc()
            pt_, pr = pb.t[b], pb.res[b]
            for kc in range(8):
                S.op('pe', lambda e, kc=kc: e.transpose(pt_[:, kc * 128:(kc + 1) * 128], y[:, kc * 128:(kc + 1) * 128], ident[:]),
                     reads=[ry, rC], writes=[pr])
            for kc in range(8):
                eng = evac_engine()
                o = dst[:, kc, dst_off:dst_off + 128]
                i_ = pt_[:, kc * 128:(kc + 1) * 128]
                sc_ = A1[:, which, kc, ci:ci + 1]
                bi_ = SH[:, which, kc, ci:ci + 1]
                if eng == 'act':
                    S.op('act', lambda e, o=o, i_=i_, sc_=sc_, bi_=bi_: e.activation(out=o, in_=i_, func=AF.Identity, scale=sc_, bias=bi_),
                         reads=[pr, rMod], writes=[dst_res])
                else:
                    S.op('dve', lambda e, o=o, i_=i_, sc_=sc_, bi_=bi_: e.tensor_scalar(out=o, in0=i_, scalar1=sc_, scalar2=bi_,
                                                                                        op0=ALU.mult, op1=ALU.add),
                         reads=[pr, rMod], writes=[dst_res])
            pb.release(b)

        def residual_epilogue(t, banks, G, rG):
            ss, rss = scr.next()
            jk, rjk = ybf.next()
            for c in range(2):
                S.op('act', lambda e, c=c: e.activation(out=jk[:, c * 512:(c + 1) * 512], in_=pf.t[banks[c]][:], func=AF.Square,
                                                        accum_out=ss[:, c:c + 1]),
                     reads=[pf.res[banks[c]]], writes=[rjk, rss])
            S.op('dve', lambda e: e.tensor_tensor(out=ss[:, 2:3], in0=ss[:, 0:1], in1=ss[:, 1:2], op=ALU.add), reads=[rss], writes=[rss])
            rs, rrs = rstd_from(ss[:, 2:3], D, [rss])
            for c in range(2):
                tm, rtm = rtmp.next()
                S.op('dve', lambda e, c=c, tm=tm: e.scalar_tensor_tensor(out=tm[:], in0=pf.t[banks[c]][:], scalar=rs[:, 0:1],
                                                                         in1=G[:, c * 512:(c + 1) * 512], op0=ALU.mult, op1=ALU.mult),
                     reads=[pf.res[banks[c]], rrs, rG], writes=[rtm])
                S.op('pool', lambda e, c=c, tm=tm: e.tensor_tensor(out=X[:, t, c * 512:(c + 1) * 512], in0=X[:, t, c * 512:(c + 1) * 512],
                                                                  in1=tm[:], op=ALU.add),
                     reads=[rtm, Xr[t]], writes=[Xr[t]])

        def gate_rows(l, chunk, gidx, wbuf, rw, G, rG, which_list, Gtmp, rGtmp):
            for h in range(2):
                S.dma('pool', wbuf[:, :, h * 512:(h + 1) * 512], wview(wmod_d[l], chunk * D + h * 512, chunk * D + (h + 1) * 512), writes=[rw])
            for i in which_list:
                S.dma('sp', G[i][:], bmod_d[l:l + 1, chunk * D:(chunk + 1) * D].partition_broadcast(128), writes=[rG[i]])
            ngb, rngb = Gtmp, rGtmp
            S.dma('sp', ngb[:], ng_d[l, gidx:gidx + 1, :].partition_broadcast(128), writes=[rngb])
            for i in which_list:
                for c in range(2):
                    b = pf.alloc()
                    for kc in range(8):
                        S.op('pe', lambda e, kc=kc, c=c, b=b, i=i: e.matmul(pf.t[b][:], lhsT=sRep[:, i, kc, :], rhs=wbuf[:, kc, c * 512:(c + 1) * 512],
                                                                       start=(kc == 0), stop=(kc == 7)),
                             reads=[rC, rw], writes=[pf.res[b]])
                    S.op('dve', lambda e, c=c, b=b, i=i: e.tensor_tensor(out=G[i][:, c * 512:(c + 1) * 512], in0=pf.t[b][:],
                                                                    in1=G[i][:, c * 512:(c + 1) * 512], op=ALU.add),
                         reads=[pf.res[b], rG[i]], writes=[rG[i]])
                    pf.release(b)
                S.op('dve', lambda e, i=i: e.tensor_tensor(out=G[i][:], in0=G[i][:], in1=ngb[:], op=ALU.mult), reads=[rG[i], rngb], writes=[rG[i]])

        def run_layer(l):
            last = (l == NLAYERS - 1)
            tiles_upd = list(range(2, NT)) if last else list(range(NT))
            which_list = [0] if last else [0, 1]
            S.barrier()
            with ExitStack() as ph:
                wm = Ring([sb(ph, "wm%d" % i, [128, 8, D], BF16) for i in range(2)])
                for ch in range(0 if os.environ.get("KDBG_SKIPPROJ") else 6):
                    w_, rw_ = wm.next()
                    for h in range(2):
                        S.dma('pool', w_[:, :, h * 512:(h + 1) * 512], wview(wmod_d[l], ch * D + h * 512, ch * D + (h + 1) * 512), writes=[rw_])
                    b = pf.alloc()
                    for fc in range(8):
                        for kc in range(8):
                            S.op('pe', lambda e, fc=fc, kc=kc, w_=w_, b=b: e.matmul(pf.t[b][:, fc * 2:fc * 2 + 2], lhsT=w_[:, kc, fc * 128:(fc + 1) * 128],
                                                                                rhs=sT[:, kc, :], start=(kc == 0), stop=(kc == 7)),
                                 reads=[rw_, rC], writes=[pf.res[b]])
                    S.op('dve', lambda e, ch=ch, b=b: e.tensor_tensor(
                        out=modcol[:, ch, :, :], in0=pf.t[b][:, 0:16].rearrange("p (f i) -> p f i", i=2),
                        in1=bmodcol[:, l, ch * 8:(ch + 1) * 8].unsqueeze(2).to_broadcast([128, 8, 2]), op=ALU.add),
                        reads=[pf.res[b], rC], writes=[rMod])
                    pf.release(b)
                for which, (shc, scc, gi) in enumerate([(0, 1, 0), (3, 4, 2)]):
                    S.op('dve', lambda e, which=which, scc=scc, gi=gi: e.scalar_tensor_tensor(
                        out=A1[:, which, :, :], in0=modcol[:, scc, :, :], scalar=1.0,
                        in1=ngcol[:, l, gi, :].unsqueeze(2).to_broadcast([128, 8, 2]), op0=ALU.add, op1=ALU.mult),
                        reads=[rMod, rC], writes=[rMod])
                    S.op('dve', lambda e, which=which, shc=shc: e.tensor_copy(out=SH[:, which, :, :], in_=modcol[:, shc, :, :]),
                         reads=[rMod], writes=[rMod])
            S.barrier()
            with ExitStack() as ph:
                qkT = sb(ph, "qkT", [128, 14, NTOK], BF16)
                qkr = [[Res() for _ in range(NT)] for _ in range(14)]
                phv = ph.enter_context(ExitStack())
                V = sb(phv, "V", [128, NT, 640], BF16)
                Vr = [Res() for _ in range(NT)]
                with ExitStack() as ph2:
                    win = sb(ph2, "win", [128, 8, 1024], BF16)
                    rwin = Res()
                    hTr = Ring([sb(ph2, "hT%d" % i, [128, 8, 128], BF16) for i in range(2)])
                    csr = Ring([sb(ph2, "cs%d" % i, [128, 128], F32) for i in range(2)])
                    cur = {}
                    gq = sb(ph2, "gq", [128, 2, 64], F32)
                    if l == 0:
                        S.dma('sp', gq[:].rearrange("p a d -> p (a d)"), qkg_d.partition_broadcast(128), writes=[rC])
                        S.op('dve', lambda e: e.tensor_scalar_mul(out=gq[:, 0, :], in0=gq[:, 0, :], scalar1=0.125),
                             reads=[rC], writes=[rC])
                    stg = Ring([sb(ph2, "stg%d" % i, [128, 512], BF16) for i in range(3)])
                    f32t = Ring([sb(ph2, "f32t%d" % i, [128, 512], F32) for i in range(3)])

                    def transpose_store(st, rst, npt, slot0, t, src_cols=None):
                        b = pb.alloc()
                        for j in range(npt):
                            c0 = j * 128 if src_cols is None else src_cols[j]
                            S.op('pe', lambda e, j=j, c0=c0, b=b: e.transpose(pb.t[b][:, j * 128:(j + 1) * 128], st[:, c0:c0 + 128], ident[:]),
                                 reads=[rst, rC], writes=[pb.res[b]])
                        eng = evac_engine()
                        o = qkT[:, slot0:slot0 + npt, t * 128:(t + 1) * 128]
                        i_ = pb.t[b][:, 0:npt * 128].rearrange("p (s n) -> p s n", n=128)
                        wr = [qkr[s][t] for s in range(slot0, slot0 + npt)]
                        if eng == 'act':
                            S.op('act', lambda e: e.activation(out=o, in_=i_, func=AF.Copy), reads=[pb.res[b]], writes=wr)
                        else:
                            S.op('dve', lambda e: e.tensor_copy(out=o, in_=i_), reads=[pb.res[b]], writes=wr)
                        pb.release(b)

                    def rope_to(st, rst, src, src_reads, nh, t, scale, dst_cols0=0):
                        n = nh * 64
                        cs_, rcs_ = cur["cs"]
                        t1, r1 = f32t.next()
                        t2, r2 = f32t.next()
                        s4 = src.rearrange("p (h d) -> p h d", d=64)
                        S.op('dve', lambda e: e.scalar_tensor_tensor(
                            out=t1[:, 0:n].rearrange("p (h d) -> p h d", d=64), in0=s4, scalar=scale,
                            in1=cs_[:, 0:64].unsqueeze(1).to_broadcast([128, nh, 64]), op0=ALU.mult, op1=ALU.mult),
                            reads=src_reads + [rcs_], writes=[r1])
                        s5 = src.rearrange("p (h a g f) -> p h a g f", a=2, g=2, f=16)
                        o5 = t2[:, 0:n].rearrange("p (h a g f) -> p h a g f", a=2, g=2, f=16)
                        sn5 = cs_[:, 64:128].rearrange("p (a g f) -> p a g f", a=2, g=2)
                        for g in range(2):
                            for a in range(2):
                                S.op('dve', lambda e, g=g, a=a: e.scalar_tensor_tensor(
                                    out=o5[:, :, a, g, :], in0=s5[:, :, a, 1 - g, :], scalar=scale,
                                    in1=sn5[:, a, g, :].unsqueeze(1).to_broadcast([128, nh, 16]), op0=ALU.mult, op1=ALU.mult),
                                    reads=src_reads + [rcs_], writes=[r2])
                        S.op('pool', lambda e: e.tensor_tensor(out=st[:, dst_cols0:dst_cols0 + n], in0=t1[:, 0:n], in1=t2[:, 0:n], op=ALU.add),
                             reads=[r1, r2], writes=[rst])

                    def headnorm(bk, c0, nh, gi):
                        n = nh * 64
                        sq, rsq = f32t.next()
                        S.op('act', lambda e: e.activation(out=sq[:, 0:n], in_=pf.t[bk][:, c0:c0 + n], func=AF.Square),
                             reads=[pf.res[bk]], writes=[rsq])
                        ssh, rssh = scr.next()
                        S.op('dve', lambda e: e.tensor_reduce(out=ssh[:, 0:nh], in_=sq[:, 0:n].rearrange("p (h d) -> p h d", d=64), axis=AX.X, op=ALU.add),
                             reads=[rsq], writes=[rssh])
                        rs, rrs = rstd_from(ssh[:, 0:nh], 64, [rssh])
                        qn, rqn = f32t.next()
                        S.op('dve', lambda e: e.tensor_tensor(out=qn[:, 0:n].rearrange("p (h d) -> p h d", d=64),
                                                              in0=pf.t[bk][:, c0:c0 + n].rearrange("p (h d) -> p h d", d=64),
                                                              in1=rs.unsqueeze(2).to_broadcast([128, nh, 64]), op=ALU.mult),
                             reads=[pf.res[bk], rrs], writes=[rqn])
                        S.op('pool', lambda e: e.tensor_tensor(out=qn[:, 0:n].rearrange("p (h d) -> p h d", d=64),
                                                               in0=qn[:, 0:n].rearrange("p (h d) -> p h d", d=64),
                                                               in1=gq[:, gi, :].unsqueeze(1).to_broadcast([128, nh, 64]), op=ALU.mult),
                             reads=[rqn, rC], writes=[rqn])
                        return qn, rqn

                    blocks = [(0, 512), (512, 768), (768, 1280), (1280, 1792), (1792, 2304)]
                    for pblocks in ([] if os.environ.get("KDBG_SKIPPROJ") else [[0, 1], [2, 3], [4]]):
                      pc0 = blocks[pblocks[0]][0]
                      pc1 = blocks[pblocks[-1]][1]
                      for cb in range((pc1 - pc0) // 256):
                          S.dma('pool', win[:, :, cb * 256:(cb + 1) * 256], wview(win_d[l], pc0 + cb * 256, pc0 + (cb + 1) * 256), writes=[rwin])
                      need_rope = (pblocks[0] == 0) or (l == 0 and pblocks[0] == 2)
                      for t in range(NT):
                        hT, rhT = hTr.next()
                        norm_mod_transpose(t, 0, hT, rhT, 0)
                        is_x = t >= 2
                        if is_x and need_rope:
                            cs_, rcs_ = csr.next()
                            S.dma('sp', cs_[:, 0:64], cos_d[:, t - 2, :], writes=[rcs_])
                            S.dma('sp', cs_[:, 64:128], sin_d[:, t - 2, :], writes=[rcs_])
                            cur["cs"] = (cs_, rcs_)
                        for bi in pblocks:
                            c0, c1 = blocks[bi]
                            n = c1 - c0
                            bk = pf.alloc()
                            for kc in range(8):
                                S.op('pe', lambda e, kc=kc, bk=bk: e.matmul(pf.t[bk][:, 0:n], lhsT=hT[:, kc, :], rhs=win[:, kc, c0 - pc0:c1 - pc0],
                                                                        start=(kc == 0), stop=(kc == 7)),
                                     reads=[rhT, rwin], writes=[pf.res[bk]])
                            pr = pf.res[bk]
                            ps = pf.t[bk]
                            if bi == 0:
                                st, rst = stg.next()
                                if l == 0:
                                    qn, rqn = headnorm(bk, 0, 8, 0)
                                    if is_x:
                                        rope_to(st, rst, qn[:, 0:512], [rqn], 8, t, 1.0)
                                    else:
                                        S.op('act', lambda e: e.activation(out=st[:, 0:512], in_=qn[:, 0:512], func=AF.Copy), reads=[rqn], writes=[rst])
                                else:
                                    if is_x:
                                        rope_to(st, rst, ps[:, 0:512], [pr], 8, t, 0.125)
                                    else:
                                        S.op('act', lambda e: e.activation(out=st[:, 0:512], in_=ps[:, 0:512], func=AF.Identity, scale=0.125), reads=[pr], writes=[rst])
                                transpose_store(st, rst, 4, 0, t)
                            elif bi == 1:
                                st, rst = stg.next()
                                if l == 0:
                                    kn, rkn = headnorm(bk, 0, 2, 1)
                                    src, srcr = kn[:, 0:128], [rkn]
                                else:
                                    src, srcr = ps[:, 0:128], [pr]
                                if is_x:
                                    rope_to(st, rst, src, srcr, 2, t, 1.0, dst_cols0=0)
                                else:
                                    S.op('act', lambda e: e.activation(out=st[:, 0:128], in_=src, func=AF.Copy), reads=srcr, writes=[rst])
                                S.op('pool', lambda e: e.tensor_copy(
                                    out=st[:, 128:384].rearrange("p (g c d) -> p g c d", g=2, c=2),
                                    in_=st[:, 0:128].rearrange("p (g d) -> p g d", g=2).unsqueeze(2).to_broadcast([128, 2, 2, 64])),
                                    reads=[rst], writes=[rst])
                                transpose_store(st, rst, 2, 8, t, src_cols=[128, 256])
                                S.op('act', lambda e: e.activation(out=V[:, t, 0:128], in_=ps[:, 128:256], func=AF.Copy), reads=[pr], writes=[Vr[t]])
                            elif bi == 2:
                                st, rst = stg.next()
                                if l == 0 and is_x:
                                    rope_to(st, rst, ps[:, 0:512], [pr], 8, t, 0.125)
                                else:
                                    S.op('act', lambda e: e.activation(out=st[:, 0:512], in_=ps[:, 0:512], func=AF.Identity, scale=0.125), reads=[pr], writes=[rst])
                                transpose_store(st, rst, 4, 4, t)
                            elif bi == 3:
                                st, rst = stg.next()
                                if l == 0 and is_x:
                                    rope_to(st, rst, ps[:, 0:512], [pr], 8, t, 1.0)
                                else:
                                    S.op('act', lambda e: e.activation(out=st[:, 0:512], in_=ps[:, 0:512], func=AF.Copy), reads=[pr], writes=[rst])
                                transpose_store(st, rst, 4, 10, t)
                            else:
                                S.op('dve', lambda e: e.tensor_copy(out=V[:, t, 128:640], in_=ps[:, 0:512]), reads=[pr], writes=[Vr[t]])
                            pf.release(bk)
                S.barrier()
                if debug_stop == (l, 'proj'):
                    return True
                with ExitStack() as ph2:
                    Pt = Ring([sb(ph2, "Pt%d" % i, [128, 512], BF16) for i in range(16 if l == 0 else 9)])
                    f32a = Ring([sb(ph2, "f32a%d" % i, [128, 512], F32) for i in range(6 if l == 0 else 4)])
                    if l == 0:
                        lamp = sb(ph2, "lamp", [128, 4, 64], F32)
                        rl = Res()
                        S.dma('sp', lamp[:].rearrange("p a d -> p (a d)"), lamp_d.partition_broadcast(128), writes=[rl])
                        lt, rlt = scr.next()
                        for j in range(2):
                            pr_, rpr_ = f32a.next()
                            S.op('dve', lambda e, j=j, pr_=pr_: e.tensor_tensor(out=pr_[:, 0:64], in0=lamp[:, 2 * j, :], in1=lamp[:, 2 * j + 1, :], op=ALU.mult),
                                 reads=[rl], writes=[rpr_])
                            S.op('dve', lambda e, j=j, pr_=pr_: e.tensor_reduce(out=lt[:, j:j + 1], in_=pr_[:, 0:64], axis=AX.X, op=ALU.add),
                                 reads=[rpr_], writes=[rlt])
                        S.op('act', lambda e: e.activation(out=lt[:, 2:4], in_=lt[:, 0:2], func=AF.Exp), reads=[rlt], writes=[rlt])
                        S.op('dve', lambda e: e.tensor_tensor(out=lt[:, 4:5], in0=lt[:, 3:4], in1=lt[:, 2:3], op=ALU.subtract), reads=[rlt], writes=[rlt])
                        S.op('dve', lambda e: e.tensor_scalar_add(out=small[:, 0:1], in0=lt[:, 4:5], scalar1=-lambda_init(0)),
                             reads=[rlt], writes=[rC])
                        S.dma('sp', small[:, 1:2], subln_d, writes=[rC])
                        S.op('dve', lambda e: e.tensor_scalar_mul(out=small[:, 2:3], in0=small[:, 1:2], scalar1=1.0 - lambda_init(0)),
                             reads=[rC], writes=[rC])
                        neglam = small[:, 0:1]
                        sgl = small[:, 2:3]
                        qblocks = [(0, 256, [0, 1], [0, 1])] + [(256 + 512 * i, 512, list(range(NT)), [2 + 4 * i + j for j in range(4)]) for i in range(4)]
                        for (tok0, n, kts, qtiles) in qblocks:
                            for pt in range(4):
                                bo = pf.alloc()
                                bd = pf.alloc()
                                for half in range(2):
                                    h = 2 * pt + half
                                    kv = h // 4
                                    hp = slice(half * 64, (half + 1) * 64)
                                    def qk_a(kt, hp=hp, kv=kv, pt=pt, half=half):
                                        bs = pf.alloc()
                                        S.op('pe', lambda e: e.matmul(pf.t[bs][:, 0:n], lhsT=qkT[hp, 8 + kv, kt * 128:(kt + 1) * 128],
                                                                      rhs=qkT[hp, pt, tok0:tok0 + n], start=True, stop=True, tile_position=(half * 64, 0)),
                                             reads=[qkr[8 + kv][kt]] + [qkr[pt][q] for q in qtiles], writes=[pf.res[bs]])
                                        p_, rp_ = Pt.next()
                                        S.op('act', lambda e: e.activation(out=p_[:, 0:n], in_=pf.t[bs][:, 0:n], func=AF.Exp),
                                             reads=[pf.res[bs]], writes=[rp_])
                                        pf.release(bs)
                                        return p_, rp_

                                    def pv_a(ki, kt, p_, rp_, hp=hp, kv=kv, half=half):
                                        S.op('pe', lambda e: e.matmul(pf.t[bo][hp, 0:n], lhsT=V[:, kt, kv * 64:(kv + 1) * 64], rhs=p_[:, 0:n],
                                                                      start=(ki == 0), stop=(ki == len(kts) - 1), tile_position=(0, half * 64)),
                                             reads=[Vr[kt], rp_], writes=[pf.res[bo]])
                                        S.op('pe', lambda e: e.matmul(pf.t[bd][hp, 0:n], lhsT=ones[:, 0:64], rhs=p_[:, 0:n],
                                                                      start=(ki == 0), stop=(ki == len(kts) - 1), tile_position=(0, half * 64)),
                                             reads=[rC, rp_], writes=[pf.res[bd]])

                                    pipeline(kts, qk_a, pv_a)
                                rr, rrr = f32a.next()
                                S.op('dve', lambda e: e.reciprocal(out=rr[:, 0:n], in_=pf.t[bd][:, 0:n]), reads=[pf.res[bd]], writes=[rrr])
                                S.op('dve', lambda e: e.tensor_tensor(out=qkT[:, pt, tok0:tok0 + n], in0=pf.t[bo][:, 0:n], in1=rr[:, 0:n], op=ALU.mult),
                                     reads=[pf.res[bo], rrr], writes=[qkr[pt][q] for q in qtiles])
                                pf.release(bo)
                                pf.release(bd)
                            for hb in range(4):
                                bo = [pf.alloc(), pf.alloc()]
                                bd = [pf.alloc(), pf.alloc()]
                                for j in range(2):
                                    hp = slice(j * 64, (j + 1) * 64)
                                    def qk_b(kt, hp=hp, j=j, hb=hb):
                                        bs = pf.alloc()
                                        S.op('pe', lambda e: e.matmul(pf.t[bs][:, 0:n], lhsT=qkT[hp, 10 + hb, kt * 128:(kt + 1) * 128],
                                                                      rhs=qkT[hp, 4 + hb, tok0:tok0 + n], start=True, stop=True, tile_position=(j * 64, 0)),
                                             reads=[qkr[10 + hb][kt]] + [qkr[4 + hb][q] for q in qtiles], writes=[pf.res[bs]])
                                        p_, rp_ = Pt.next()
                                        S.op('act', lambda e: e.activation(out=p_[:, 0:n], in_=pf.t[bs][:, 0:n], func=AF.Exp),
                                             reads=[pf.res[bs]], writes=[rp_])
                                        pf.release(bs)
                                        return p_, rp_

                                    def pv_b(ki, kt, p_, rp_, j=j, hb=hb):
                                        S.op('pe', lambda e: e.matmul(pf.t[bo[j]][:, 0:n], lhsT=V[:, kt, 128 + hb * 128:128 + (hb + 1) * 128], rhs=p_[:, 0:n],
                                                                      start=(ki == 0), stop=(ki == len(kts) - 1)),
                                             reads=[Vr[kt], rp_], writes=[pf.res[bo[j]]])
                                        S.op('pe', lambda e: e.matmul(pf.t[bd[j]][:, 0:n], lhsT=ones[:, :], rhs=p_[:, 0:n],
                                                                      start=(ki == 0), stop=(ki == len(kts) - 1)),
                                             reads=[rC, rp_], writes=[pf.res[bd[j]]])

                                    pipeline(kts, qk_b, pv_b)
                                oo = []
                                for j in range(2):
                                    rr, rrr = f32a.next()
                                    S.op('dve', lambda e, j=j, rr=rr: e.reciprocal(out=rr[:, 0:n], in_=pf.t[bd[j]][:, 0:n]), reads=[pf.res[bd[j]]], writes=[rrr])
                                    o_, ro_ = f32a.next()
                                    S.op('dve', lambda e, j=j, rr=rr, o_=o_: e.tensor_tensor(out=o_[:, 0:n], in0=pf.t[bo[j]][:, 0:n], in1=rr[:, 0:n], op=ALU.mult),
                                         reads=[pf.res[bo[j]], rrr], writes=[ro_])
                                    oo.append((o_, ro_))
                                for j in range(2):
                                    pf.release(bo[j])
                                    pf.release(bd[j])
                                ob, rob = f32a.next()
                                S.op('dve', lambda e: e.scalar_tensor_tensor(out=ob[:, 0:n], in0=oo[1][0][:, 0:n], scalar=neglam, in1=oo[0][0][:, 0:n],
                                                                              op0=ALU.mult, op1=ALU.add),
                                     reads=[oo[0][1], oo[1][1], rC], writes=[rob])
                                sq, rsq = Pt.next()
                                S.op('pool', lambda e: e.tensor_tensor(out=sq[:, 0:n], in0=ob[:, 0:n], in1=ob[:, 0:n], op=ALU.mult), reads=[rob], writes=[rsq])
                                bq = pf.alloc()
                                S.op('pe', lambda e: e.matmul(pf.t[bq][:, 0:n], lhsT=ones[:, :], rhs=sq[:, 0:n], start=True, stop=True),
                                     reads=[rC, rsq], writes=[pf.res[bq]])
                                rs_, rrs_ = f32a.next()
                                S.op('dve', lambda e: e.tensor_scalar(out=rs_[:, 0:n], in0=pf.t[bq][:, 0:n], scalar1=1.0 / 128, scalar2=EPS, op0=ALU.mult, op1=ALU.add),
                                     reads=[pf.res[bq]], writes=[rrs_])
                                pf.release(bq)
                                S.op('act', lambda e: e.activation(out=rs_[:, 0:n], in_=rs_[:, 0:n], func=AF.Sqrt), reads=[rrs_], writes=[rrs_])
                                S.op('dve', lambda e: e.reciprocal(out=rs_[:, 0:n], in_=rs_[:, 0:n]), reads=[rrs_], writes=[rrs_])
                                S.op('dve', lambda e: e.scalar_tensor_tensor(out=qkT[:, 4 + hb, tok0:tok0 + n], in0=ob[:, 0:n], scalar=sgl, in1=rs_[:, 0:n],
                                                                             op0=ALU.mult, op1=ALU.mult),
                                     reads=[rob, rrs_, rC], writes=[qkr[4 + hb][q] for q in qtiles])
                    else:
                        tbl = sb(ph2, "tbl", [128, 4, 16, 64], F32)
                        rtbl = Res()
                        qzr = Ring([sb(ph2, "qz%d" % i, [128, 2, 128], BF16) for i in range(4)])
                        for qz_ in qzr.t:
                            S.op('pool', lambda e, qz_=qz_: e.memset(qz_[:], 0.0), writes=[rC])

                        def make_qz(slot, t):
                            qz_, rqz_ = qzr.next()
                            S.op('pool', lambda e: e.tensor_copy(out=qz_[0:64, 0, :], in_=qkT[0:64, slot, t * 128:(t + 1) * 128]),
                                 reads=[qkr[slot][t], rC], writes=[rqz_])
                            S.op('dve', lambda e: e.tensor_copy(out=qz_[64:128, 1, :], in_=qkT[64:128, slot, t * 128:(t + 1) * 128]),
                                 reads=[qkr[slot][t], rC], writes=[rqz_])
                            return qz_, rqz_
                        mprev = sb(ph2, "mprev", [128, 128], BF16)
                        mnext = sb(ph2, "mnext", [128, 128], BF16)
                        S.dma('pool', mprev[:], mprev_d, writes=[rC])
                        S.dma('pool', mnext[:], mnext_d, writes=[rC])
                        esink = sb(ph2, "esink", [128, 8], F32)
                        S.dma('sp', esink[:], sink_d.partition_broadcast(128), writes=[rC])
                        S.op('act', lambda e: e.activation(out=esink[:], in_=esink[:], func=AF.Exp), reads=[rC], writes=[rC])
                        for i in range(int(os.environ.get("KDBG_NI", "16"))):
                            t = i + 2
                            tok0 = t * 128
                            for g in range(2):
                                klist = [(0, None), (1, None)]
                                if i >= 1:
                                    klist.append((t - 1, mprev))
                                klist.append((t, None))
                                if i <= 14:
                                    klist.append((t + 1, mnext))
                                pts = []
                                qzs = [make_qz(2 * g + j, t) for j in range(2)]
                                for (kt, msk) in klist:
                                    bs = pf.alloc()
                                    for hq in range(4):
                                        half = hq % 2
                                        qz_, rqz_ = qzs[hq // 2]
                                        S.op('pe', lambda e, bs=bs, kt=kt, hq=hq, half=half, qz_=qz_: e.matmul(
                                            pf.t[bs][:, hq * 128:(hq + 1) * 128], lhsT=qkT[:, 8 + g, kt * 128:(kt + 1) * 128],
                                            rhs=qz_[:, half, :], start=True, stop=True),
                                            reads=[qkr[8 + g][kt], rqz_], writes=[pf.res[bs]])
                                    p_, rp_ = Pt.next()
                                    S.op('act', lambda e, bs=bs, p_=p_: e.activation(out=p_[:], in_=pf.t[bs][:], func=AF.Exp), reads=[pf.res[bs]], writes=[rp_])
                                    pf.release(bs)
                                    if msk is not None:
                                        S.op('pool', lambda e, p_=p_, msk=msk: e.tensor_tensor(
                                            out=p_[:].rearrange("p (h q) -> p h q", h=4), in0=p_[:].rearrange("p (h q) -> p h q", h=4),
                                            in1=msk[:].unsqueeze(1).to_broadcast([128, 4, 128]), op=ALU.mult), reads=[rp_, rC], writes=[rp_])
                                    pts.append((kt, p_, rp_))
                                bo = pf.alloc()
                                bd = pf.alloc()
                                for hq in range(4):
                                    ptl = hq // 2
                                    half = hq % 2
                                    hp = slice(half * 64, (half + 1) * 64)
                                    for ki, (kt, p_, rp_) in enumerate(pts):
                                        S.op('pe', lambda e, kt=kt, p_=p_, ki=ki, hq=hq, ptl=ptl, hp=hp, half=half: e.matmul(
                                            pf.t[bo][hp, ptl * 128:(ptl + 1) * 128], lhsT=V[:, kt, g * 64:(g + 1) * 64], rhs=p_[:, hq * 128:(hq + 1) * 128],
                                            start=(ki == 0), stop=(ki == len(pts) - 1), tile_position=(0, half * 64)),
                                            reads=[Vr[kt], rp_], writes=[pf.res[bo]])
                                        S.op('pe', lambda e, p_=p_, ki=ki, hq=hq, ptl=ptl, hp=hp, half=half: e.matmul(
                                            pf.t[bd][hp, ptl * 128:(ptl + 1) * 128], lhsT=ones[:, 0:64], rhs=p_[:, hq * 128:(hq + 1) * 128],
                                            start=(ki == 0), stop=(ki == len(pts) - 1), tile_position=(0, half * 64)),
                                            reads=[rC, rp_], writes=[pf.res[bd]])
                                rr, rrr = f32a.next()
                                for hq in range(4):
                                    ptl = hq // 2
                                    half = hq % 2
                                    hp = slice(half * 64, (half + 1) * 64)
                                    h = 4 * g + hq
                                    S.op('dve', lambda e, hp=hp, ptl=ptl, h=h: e.tensor_scalar_add(
                                        out=rr[hp, ptl * 128:(ptl + 1) * 128], in0=pf.t[bd][hp, ptl * 128:(ptl + 1) * 128],
                                        scalar1=esink[hp, h:h + 1]), reads=[pf.res[bd], rC], writes=[rrr])
                                S.op('dve', lambda e: e.reciprocal(out=rr[:, 0:256], in_=rr[:, 0:256]), reads=[rrr], writes=[rrr])
                                S.op('dve', lambda e: e.tensor_tensor(out=qkT[:, 2 * g:2 * g + 2, tok0:tok0 + 128],
                                                                      in0=pf.t[bo][:, 0:256].rearrange("p (s q) -> p s q", s=2),
                                                                      in1=rr[:, 0:256].rearrange("p (s q) -> p s q", s=2), op=ALU.mult),
                                     reads=[pf.res[bo], rrr], writes=[qkr[2 * g][t], qkr[2 * g + 1][t]])
                                pf.release(bo)
                                pf.release(bd)
                        if debug_stop == (l, 'c'):
                            return True
                        for hg in range(2):
                            S.dma('sp', tbl[:], tbl_d[:, hg * 4:(hg + 1) * 4], writes=[rtbl])
                            for i in range(16):
                                t = i + 2
                                tok0 = t * 128
                                r0, r1 = 2 * i, 2 * i + 1
                                rs0 = min(max(r0 - 4, 0), 24)
                                rs1 = min(max(r1 - 4, 0), 24)
                                kx_lo = rs0 // 2
                                kx_hi = (rs1 + 8 + 1) // 2
                                klist = [(0, None), (1, None)]
                                for kx in range(kx_lo, kx_hi):
                                    jj = {}
                                    for kr in range(2):
                                        for qr in range(2):
                                            krow = 2 * kx + kr
                                            qrow = 2 * i + qr
                                            rs_q = min(max(qrow - 4, 0), 24)
                                            if rs_q <= krow < rs_q + 8:
                                                jj[(kr, qr)] = krow - qrow + 7
                                            else:
                                                jj[(kr, qr)] = 15
                                    klist.append((kx + 2, jj))
                                pts = []
                                qzs = [make_qz(4 + hg * 2 + j, t) for j in range(2)]
                                for (kt, jj) in klist:
                                    bs = pf.alloc()
                                    for hq in range(4):
                                        h = hg * 4 + hq
                                        half = h % 2
                                        qz_, rqz_ = qzs[hq // 2]
                                        S.op('pe', lambda e, bs=bs, kt=kt, hq=hq, h=h, half=half, qz_=qz_: e.matmul(
                                            pf.t[bs][:, hq * 128:(hq + 1) * 128], lhsT=qkT[:, 10 + h // 2, kt * 128:(kt + 1) * 128],
                                            rhs=qz_[:, half, :], start=True, stop=True),
                                            reads=[qkr[10 + h // 2][kt], rqz_], writes=[pf.res[bs]])
                                    p_, rp_ = Pt.next()
                                    if jj is None:
                                        S.op('act', lambda e, bs=bs, p_=p_: e.activation(out=p_[:], in_=pf.t[bs][:], func=AF.Exp), reads=[pf.res[bs]], writes=[rp_])
                                    else:
                                        sbf, rsbf = f32a.next()
                                        for kr in range(2):
                                            for qr in range(2):
                                                kp = slice(kr * 64, (kr + 1) * 64)
                                                S.op('dve', lambda e, bs=bs, kp=kp, qr=qr, kr=kr, jj=jj, sbf=sbf: e.tensor_tensor(
                                                    out=sbf[kp, :].rearrange("p (h q) -> p h q", h=4)[:, :, qr * 64:(qr + 1) * 64],
                                                    in0=pf.t[bs][kp, :].rearrange("p (h q) -> p h q", h=4)[:, :, qr * 64:(qr + 1) * 64],
                                                    in1=tbl[kp, :, jj[(kr, qr)], :], op=ALU.add),
                                                    reads=[pf.res[bs], rtbl], writes=[rsbf])
                                        S.op('act', lambda e, sbf=sbf, p_=p_: e.activation(out=p_[:], in_=sbf[:], func=AF.Exp), reads=[rsbf], writes=[rp_])
                                    pf.release(bs)
                                    pts.append((kt, p_, rp_))
                                bo = pf.alloc()
                                bd = pf.alloc()
                                for hq in range(4):
                                    h = hg * 4 + hq
                                    ptl = hq // 2
                                    half = hq % 2
                                    hp = slice(half * 64, (half + 1) * 64)
                                    for ki, (kt, p_, rp_) in enumerate(pts):
                                        S.op('pe', lambda e, kt=kt, p_=p_, ki=ki, hq=hq, ptl=ptl, hp=hp, half=half, h=h: e.matmul(
                                            pf.t[bo][hp, ptl * 128:(ptl + 1) * 128], lhsT=V[:, kt, 128 + h * 64:128 + (h + 1) * 64], rhs=p_[:, hq * 128:(hq + 1) * 128],
                                            start=(ki == 0), stop=(ki == len(pts) - 1), tile_position=(0, half * 64)),
                                            reads=[Vr[kt], rp_], writes=[pf.res[bo]])
                                        S.op('pe', lambda e, p_=p_, ki=ki, hq=hq, ptl=ptl, hp=hp, half=half: e.matmul(
                                            pf.t[bd][hp, ptl * 128:(ptl + 1) * 128], lhsT=ones[:, 0:64], rhs=p_[:, hq * 128:(hq + 1) * 128],
                                            start=(ki == 0), stop=(ki == len(pts) - 1), tile_position=(0, half * 64)),
                                            reads=[rC, rp_], writes=[pf.res[bd]])
                                rr, rrr = f32a.next()
                                S.op('dve', lambda e: e.reciprocal(out=rr[:, 0:256], in_=pf.t[bd][:, 0:256]), reads=[pf.res[bd]], writes=[rrr])
                                s0 = 4 + hg * 2
                                S.op('dve', lambda e: e.tensor_tensor(out=qkT[:, s0:s0 + 2, tok0:tok0 + 128],
                                                                      in0=pf.t[bo][:, 0:256].rearrange("p (s q) -> p s q", s=2),
                                                                      in1=rr[:, 0:256].rearrange("p (s q) -> p s q", s=2), op=ALU.mult),
                                     reads=[pf.res[bo], rrr], writes=[qkr[s0][t], qkr[s0 + 1][t]])
                                pf.release(bo)
                                pf.release(bd)
                S.barrier()
                phv.close()
                with ExitStack() as ph2:
                    wo = sb(ph2, "wo", [128, 8, D], BF16)
                    rwo = Res()
                    G = [sb(ph2, "G%d" % i, [128, D], F32) for i in which_list]
                    rG = [Res() for _ in which_list]
                    Gtmp = sb(ph2, "Gtmp", [128, D], F32)
                    rGtmp = Res()
                    gate_rows(l, 2, 1, wo, rwo, G, rG, which_list, Gtmp, rGtmp)
                    for h in range(2):
                        S.dma('pool', wo[:, :, h * 512:(h + 1) * 512], wview(wout_d[l], h * 512, (h + 1) * 512), writes=[rwo])
                    for t in tiles_upd:
                        ci = 1 if t < 2 else 0
                        banks = [pf.alloc(), pf.alloc()]
                        for c in range(2):
                            for s in range(8):
                                S.op('pe', lambda e, c=c, s=s: e.matmul(pf.t[banks[c]][:], lhsT=qkT[:, s, t * 128:(t + 1) * 128],
                                                                      rhs=wo[:, s, c * 512:(c + 1) * 512], start=(s == 0), stop=(s == 7)),
                                     reads=[qkr[s][t], rwo], writes=[pf.res[banks[c]]])
                        residual_epilogue(t, banks, G[ci], rG[ci])
                        pf.release(banks[0])
                        pf.release(banks[1])
            S.barrier()
            if debug_stop == (l, 'attn'):
                return True
            with ExitStack() as ph:
                W2 = sb(ph, "W2", [128, 32, D], BF16)
                rW2 = Res()
                G = [sb(ph, "G2_%d" % i, [128, D], F32) for i in which_list]
                rG = [Res() for _ in which_list]
                hidT = sb(ph, "hidT", [128, 32, 384], BF16)
                rhid = Res()
                with ExitStack() as phg:
                    Gtmp = sb(phg, "Gtmp2", [128, D], F32)
                    rGtmp = Res()
                    gate_rows(l, 5, 3, W2[:, 0:8, :], rW2, G, rG, which_list, Gtmp, rGtmp)
                    S.barrier()
                w2v = w2_d[l].rearrange("(hc p) n -> p hc n", p=128)
                for q in range(8):
                    S.dma('pool', W2[:, q * 4:(q + 1) * 4, :], w2v[:, q * 4:(q + 1) * 4, :], writes=[rW2])
                hT2r = Ring([sb(ph, "hT2_%d" % i, [128, 8, 384], BF16) for i in range(1)])
                w1r = Ring([sb(ph, "w1_%d" % i, [128, 8, 256], BF16) for i in range(3)])
                relu_t = Ring([sb(ph, "relu%d" % i, [128, 384], F32) for i in range(2)])
                blocks = [tiles_upd[i:i + 3] for i in range(0, len(tiles_upd), 3)]
                for blk in blocks:
                    nb = len(blk) * 128
                    hT2, rh2 = hT2r.next()
                    for j, t in enumerate(blk):
                        norm_mod_transpose(t, 1, hT2, rh2, j * 128)
                    for p in range(16):
                        w1, rw1 = w1r.next()
                        S.dma('pool', w1[:], wview(w1_d[l], p * 256, (p + 1) * 256), writes=[rw1])
                        for h2 in range(2):
                            hc = p * 2 + h2
                            bk = pf.alloc()
                            for kc in range(8):
                                S.op('pe', lambda e, kc=kc, bk=bk, h2=h2, w1=w1: e.matmul(pf.t[bk][:, 0:nb], lhsT=w1[:, kc, h2 * 128:(h2 + 1) * 128],
                                                                                   rhs=hT2[:, kc, 0:nb], start=(kc == 0), stop=(kc == 7)),
                                     reads=[rw1, rh2], writes=[pf.res[bk]])
                            rl_, rrl_ = relu_t.next()
                            S.op('act', lambda e, bk=bk, rl_=rl_: e.activation(out=rl_[:, 0:nb], in_=pf.t[bk][:, 0:nb], func=AF.Relu),
                                 reads=[pf.res[bk]], writes=[rrl_])
                            pf.release(bk)
                            S.op('pool', lambda e, hc=hc, rl_=rl_: e.tensor_tensor(out=hidT[:, hc, 0:nb], in0=rl_[:, 0:nb], in1=rl_[:, 0:nb], op=ALU.mult),
                                 reads=[rrl_], writes=[rhid])
                    for j, t in enumerate(blk):
                        ci = 1 if t < 2 else 0
                        banks = [pf.alloc(), pf.alloc()]
                        for c in range(2):
                            for hc in range(32):
                                S.op('pe', lambda e, c=c, hc=hc, j=j: e.matmul(pf.t[banks[c]][:], lhsT=hidT[:, hc, j * 128:(j + 1) * 128],
                                                                            rhs=W2[:, hc, c * 512:(c + 1) * 512], start=(hc == 0), stop=(hc == 31)),
                                     reads=[rhid, rW2], writes=[pf.res[banks[c]]])
                        residual_epilogue(t, banks, G[ci], rG[ci])
                        pf.release(banks[0])
                        pf.release(banks[1])
            if debug_stop == (l, 'mlp'):
                return True

        for l in layers:
            if run_layer(l):
                break
        S.barrier()
        yv = y_d.rearrange("(t p) d -> p t d", p=128)
        ro = Res()
        if out_all:
            S.dma('sp', yv[:, 0:2, :], X[:, 0:2, :], reads=Xr[0:2], writes=[ro])
        yo = 2 if out_all else 0
        for i in range(4):
            S.dma('sp', yv[:, yo + 4 * i:yo + 4 * i + 4, :], X[:, 2 + 4 * i:6 + 4 * i, :], reads=Xr[2 + 4 * i:6 + 4 * i], writes=[ro])
        S.barrier()
    return nc


def _host_constants():
    ident = np.eye(128, dtype=np.float32)
    t = np.arange(T)
    pos = np.stack([t // 64, t % 64], -1).astype(np.float32)
    inv = (np.float32(10000.0) ** (-np.arange(16, dtype=np.float32) / np.float32(16))).astype(np.float32)
    ang = pos[..., None] * inv
    cos = np.cos(ang).astype(np.float32)
    sin = np.sin(ang).astype(np.float32)
    C = np.stack([cos, cos], axis=2)
    Sg = np.stack([-sin, sin], axis=2)
    cos_t = np.ascontiguousarray(C.reshape(16, 128, 64).transpose(1, 0, 2))
    sin_t = np.ascontiguousarray(Sg.reshape(16, 128, 64).transpose(1, 0, 2))
    a = np.arange(128)[:, None]
    b = np.arange(128)[None, :]
    mprev = (b <= a).astype(np.float32)
    mnext = (a <= b).astype(np.float32)
    return dict(ident=ident, cos_t=cos_t, sin_t=sin_t, mprev=mprev, mnext=mnext)


def _tbl_from_rpb(rpb):
    kc = (np.arange(128) % 64)[:, None]
    qc = np.arange(64)[None, :]
    cs = np.clip(qc - 8, 0, 48)
    valid = (kc >= cs) & (kc < cs + 16)
    dc = np.clip(kc - qc, -15, 15) + 15
    ext = np.concatenate([rpb.reshape(8, -1), np.full((8, 1), NEGV, np.float32)], axis=1)
    idx = np.empty((128, 16, 64), np.int64)
    for j in range(15):
        idx[:, j, :] = np.where(valid, j * 31 + dc, 465)
    idx[:, 15, :] = 465
    tbl = ext[:, idx]
    return np.ascontiguousarray(tbl.transpose(1, 0, 2, 3)).astype(np.float32)


_CACHE = {}
_MODE = "fused"


def kernel(x, c, ctx, c_ctx, w_mod, b_mod, norm_g, w_in, w_out, w_mlp_in, w_mlp_out,
           qk_norm_a, diff_lambda, diff_subln, sink_c, rpb_d, _debug_stop=None):
    f = lambda a: np.ascontiguousarray(np.asarray(a, dtype=np.float32))
    x, c, ctx, c_ctx = f(x), f(c), f(ctx), f(c_ctx)
    w_mod, b_mod, norm_g, w_in, w_out, w_mlp_in, w_mlp_out = map(f, (w_mod, b_mod, norm_g, w_in, w_out, w_mlp_in, w_mlp_out))
    qk_norm_a, diff_lambda, diff_subln, sink_c, rpb_d = map(f, (qk_norm_a, diff_lambda, diff_subln, sink_c, rpb_d))
    def get_nc(layers):
        key = ("nc", layers, _debug_stop)
        if key not in _CACHE:
            _CACHE[key] = build_program(layers, _debug_stop)
        return _CACHE[key]
    consts = _host_constants()
    shared = dict(
        w_mod=w_mod, b_mod=b_mod, norm_g=norm_g, w_in=w_in, w_out=w_out, w_mlp_in=w_mlp_in, w_mlp_out=w_mlp_out,
        bmod_col=f(b_mod.reshape(2, 48, 128).transpose(2, 0, 1)),
        ng_col=f(norm_g.reshape(2, 4, 8, 128).transpose(3, 0, 1, 2)),
        qkg=f(qk_norm_a[0].reshape(1, 128)), lamp=f(diff_lambda[0].reshape(1, 256)),
        subln_col=f(diff_subln[0].reshape(128, 1)), sink=f(sink_c[0].reshape(1, 8)),
        tbl=_tbl_from_rpb(rpb_d[0]), **consts)
    def launch(layers, xs, cs):
        in_maps = []
        for b in range(8):
            m = dict(shared)
            m["x"] = f(xs[b])
            m["ctx"] = f(cs[b])
            m["ccol"] = f(np.stack([c[b].reshape(8, 128).T, c_ctx.reshape(8, 128).T], -1))
            in_maps.append(m)
        res = run_bass_kernel_spmd(get_nc(layers), in_maps, core_ids=list(range(8)))
        return [r["y"] for r in res.results]

    if _debug_stop is not None:
        ys = launch((_debug_stop[0],), x, ctx)
        return np.stack(ys, 0).astype(np.float32)
    if _MODE == "fused":
        ys = launch((0, 1), x, ctx)
        return np.stack(ys, 0).astype(np.float32)
    ys = launch((0,), x, ctx)
    ys = launch((1,), [y[LC:] for y in ys], [y[:LC] for y in ys])
    return np.stack(ys, 0).astype(np.float32)
```

```python
import numpy as np
import math
import os
from contextlib import ExitStack
import concourse.bass as bass
import concourse.mybir as mybir
from concourse.bass_utils import run_bass_kernel_spmd

F32 = mybir.dt.float32
BF16 = mybir.dt.bfloat16
ALU = mybir.AluOpType
AF = mybir.ActivationFunctionType
AX = mybir.AxisListType

D = 1024
T = 2048
LC = 256
NT = 18
NTOK = NT * 128
EPS = 1e-6
NEGV = -30000.0
NLAYERS = 2


class _Stop(Exception):
    pass


class Res:
    __slots__ = ("w", "r")

    def __init__(self):
        self.w = None
        self.r = {}


class Sched:
    NDMA = 24

    def __init__(self, nc, es):
        self.nc = nc
        self.eng = dict(pe=nc.tensor, dve=nc.vector, act=nc.scalar, pool=nc.gpsimd, sp=nc.sync)
        self.esem = {k: es.enter_context(nc.semaphore("s_" + k)) for k in self.eng}
        self.ecnt = {k: 0 for k in self.eng}
        self.known = {k: {} for k in self.eng}
        self.dsem = [es.enter_context(nc.semaphore("d%d" % i)) for i in range(self.NDMA)]
        self.dcnt = [0] * self.NDMA
        half = self.NDMA // 2
        self.dq = {'pool': list(range(0, half)), 'sp': list(range(half, self.NDMA))}
        self.dnext = {'pool': 0, 'sp': 0}

    def _wait(self, e, tok):
        key, val = tok
        if key == 'pe' and e == 'pe':
            return
        if self.known[e].get(key, 0) >= val:
            return
        sem = self.esem[key] if isinstance(key, str) else self.dsem[key]
        self.eng[e].wait_ge(sem, val)
        self.known[e][key] = val

    def _deps(self, e, reads, writes):
        for r in reads:
            if r.w is not None:
                self._wait(e, r.w)
        for w in writes:
            if w.w is not None:
                self._wait(e, w.w)
            for k, v in w.r.items():
                self._wait(e, (k, v))

    def _commit(self, tok, reads, writes):
        k, v = tok
        for r in reads:
            if r.r.get(k, 0) < v:
                r.r[k] = v
        for w in writes:
            w.w = tok
            w.r = {}

    def op(self, e, fn, reads=(), writes=()):
        self._deps(e, reads, writes)
        ins = fn(self.eng[e])
        self.ecnt[e] += 1
        ins.then_inc(self.esem[e], 1)
        self._commit((e, self.ecnt[e]), reads, writes)
        return ins

    def dma(self, e, out, in_, reads=(), writes=()):
        self._deps(e, reads, writes)
        q = self.dq[e]
        i = q[self.dnext[e]]
        self.dnext[e] = (self.dnext[e] + 1) % len(q)
        if self.dcnt[i] > 0:
            self._wait(e, (i, self.dcnt[i]))
        ins = self.eng[e].dma_start(out=out, in_=in_)
        self.dcnt[i] += 16
        ins.then_inc(self.dsem[i], 16)
        self._commit((i, self.dcnt[i]), reads, writes)
        return ins

    def barrier(self):
        for e in self.eng:
            for k in self.eng:
                if k != e and self.ecnt[k] > 0:
                    self._wait(e, (k, self.ecnt[k]))
            for i in range(self.NDMA):
                if self.dcnt[i] > 0:
                    self._wait(e, (i, self.dcnt[i]))


class Banks:
    def __init__(self, tensors):
        self.t = tensors
        self.res = [Res() for _ in tensors]
        self.free = list(range(len(tensors)))

    def alloc(self):
        i = self.free.pop(0)
        return i

    def release(self, i):
        self.free.append(i)


class Ring:
    def __init__(self, tiles):
        self.t = tiles
        self.res = [Res() for _ in tiles]
        self.i = 0

    def next(self):
        i = self.i
        self.i = (i + 1) % len(self.t)
        return self.t[i], self.res[i]


def lambda_init(layer):
    return 0.8 - 0.6 * math.exp(-0.3 * layer)


def build_program(layers=(0, 1), debug_stop=None):
    nc = bass.Bass("TRN2", target_bir_lowering=False)
    dt_in = lambda name, shape: nc.dram_tensor(name, shape, F32, kind="ExternalInput").ap()
    x_d = dt_in("x", [T, D])
    ctx_d = dt_in("ctx", [LC, D])
    ccol_d = dt_in("ccol", [128, 8, 2])
    wmod_d = dt_in("w_mod", [2, D, 6 * D])
    bmodcol_d = dt_in("bmod_col", [128, 2, 48])
    bmod_d = dt_in("b_mod", [2, 6 * D])
    ngcol_d = dt_in("ng_col", [128, 2, 4, 8])
    ng_d = dt_in("norm_g", [2, 4, D])
    win_d = dt_in("w_in", [2, D, 2304])
    wout_d = dt_in("w_out", [2, D, D])
    w1_d = dt_in("w_mlp_in", [2, D, 4 * D])
    w2_d = dt_in("w_mlp_out", [2, 4 * D, D])
    qkg_d = dt_in("qkg", [1, 128])
    lamp_d = dt_in("lamp", [1, 256])
    subln_d = dt_in("subln_col", [128, 1])
    sink_d = dt_in("sink", [1, 8])
    tbl_d = dt_in("tbl", [128, 8, 16, 64])
    ident_d = dt_in("ident", [128, 128])
    cos_d = dt_in("cos_t", [128, 16, 64])
    sin_d = dt_in("sin_t", [128, 16, 64])
    mprev_d = dt_in("mprev", [128, 128])
    mnext_d = dt_in("mnext", [128, 128])
    out_all = (layers[-1] != NLAYERS - 1)
    y_d = nc.dram_tensor("y", [NTOK if out_all else T, D], F32, kind="ExternalOutput").ap()

    with ExitStack() as es:
        S = Sched(nc, es)

        uid = [0]

        def sb(scope, name, shape, dt):
            uid[0] += 1
            return scope.enter_context(nc.sbuf_tensor("sb%d_%s" % (uid[0], name), shape, dt))

        X = sb(es, "X", [128, NT, D], F32)
        Xr = [Res() for _ in range(NT)]
        ident = sb(es, "ident", [128, 128], BF16)
        ones = sb(es, "ones", [128, 128], BF16)
        sT = sb(es, "sT", [128, 8, 2], BF16)
        sRep = sb(es, "sRep", [128, 2, 8, 128], BF16)
        small = sb(es, "small", [128, 64], F32)
        modcol = sb(es, "modcol", [128, 6, 8, 2], F32)
        A1 = sb(es, "A1", [128, 2, 8, 2], F32)
        SH = sb(es, "SH", [128, 2, 8, 2], F32)
        bmodcol = sb(es, "bmodcol", [128, 2, 48], F32)
        ngcol = sb(es, "ngcol", [128, 2, 4, 8], F32)
        rC = Res()
        rMod = Res()
        sc_t = [sb(es, "sc%d" % i, [128, 16], F32) for i in range(6)]
        scr = Ring(sc_t)
        ybf = Ring([sb(es, "ybf%d" % i, [128, D], BF16) for i in range(2)])
        rtmp = Ring([sb(es, "rtmp%d" % i, [128, 512], F32) for i in range(2)])
        pf = Banks([es.enter_context(nc.psum_tensor("pf%d" % i, [128, 512], F32)) for i in range(6)])
        pb = Banks([es.enter_context(nc.psum_tensor("pb%d" % i, [128, 1024], BF16)) for i in range(2)])

        S.dma('pool', ident[:], ident_d, writes=[rC])
        S.op('dve', lambda e: e.memset(ones[:], 1.0), writes=[rC])
        S.dma('sp', bmodcol[:], bmodcol_d, writes=[rC])
        S.dma('sp', ngcol[:], ngcol_d, writes=[rC])
        cc, rcc = scr.next()
        S.dma('sp', cc[:, 0:16].rearrange("p (k i) -> p k i", i=2), ccol_d, writes=[rcc])
        sf, rsf = scr.next()
        S.op('act', lambda e: e.activation(out=sf[:, 0:16], in_=cc[:, 0:16], func=AF.Silu), reads=[rcc], writes=[rsf])
        S.op('dve', lambda e: e.tensor_copy(out=sT[:], in_=sf[:, 0:16].rearrange("p (k i) -> p k i", i=2)), reads=[rsf], writes=[rC])
        for i in range(2):
            S.op('dve', lambda e, i=i: e.tensor_copy(
                out=sRep[:, i, :, :],
                in_=sf[:, 0:16].rearrange("p (k i) -> p k i", i=2)[:, :, i:i + 1].to_broadcast([128, 8, 128])),
                reads=[rsf], writes=[rC])
        S.dma('sp', X[:, 0:2, :], ctx_d.rearrange("(t p) d -> p t d", p=128), writes=Xr[0:2])
        xv = x_d.rearrange("(t p) d -> p t d", p=128)
        for i in range(4):
            S.dma('sp', X[:, 2 + 4 * i:6 + 4 * i, :], xv[:, 4 * i:4 * i + 4, :], writes=Xr[2 + 4 * i:6 + 4 * i])

        def wview(ap2d, c0, c1):
            return ap2d[:, c0:c1].rearrange("(kc p) n -> p kc n", p=128)

        def rstd_from(ss_ap, n, reads):
            m = ss_ap.shape[1]
            t, r = scr.next()
            S.op('dve', lambda e: e.tensor_scalar(out=t[:, 0:m], in0=ss_ap, scalar1=1.0 / n, scalar2=EPS,
                                                  op0=ALU.mult, op1=ALU.add), reads=reads, writes=[r])
            S.op('act', lambda e: e.activation(out=t[:, 0:m], in_=t[:, 0:m], func=AF.Sqrt), reads=[r], writes=[r])
            S.op('dve', lambda e: e.reciprocal(out=t[:, 0:m], in_=t[:, 0:m]), reads=[r], writes=[r])
            return t[:, 0:m], r

        def pipeline(kts, qk_fn, pv_fn, depth=1):
            pend = []
            for ki, kt in enumerate(kts):
                p_, rp_ = qk_fn(kt)
                pend.append((ki, kt, p_, rp_))
                if len(pend) > depth:
                    pv_fn(*pend.pop(0))
            while pend:
                pv_fn(*pend.pop(0))

        cnt = {"ev": 0}

        def evac_engine():
            cnt["ev"] += 1
            return 'act' if cnt["ev"] % 2 == 0 else 'dve'

        def norm_mod_transpose(t, which, dst, dst_res, dst_off):
            ci = 1 if t < 2 else 0
            ss, rss = scr.next()
            y, ry = ybf.next()
            S.op('act', lambda e: e.activation(out=y[:], in_=X[:, t, :], func=AF.Square, accum_out=ss[:, 0:1]),
                 reads=[Xr[t]], writes=[ry, rss])
            rs, rrs = rstd_from(ss[:, 0:1], D, [rss])
            S.op('dve', lambda e: e.tensor_scalar_mul(out=y[:], in0=X[:, t, :], scalar1=rs[:, 0:1]), reads=[Xr[t], rrs], writes=[ry])
            b = pb.alloc()
            pt_, pr = pb.t[b], pb.res[b]
            for kc in range(8):
                S.op('pe', lambda e, kc=kc: e.transpose(pt_[:, kc * 128:(kc + 1) * 128], y[:, kc * 128:(kc + 1) * 128], ident[:]),
                     reads=[ry, rC], writes=[pr])
            for kc in range(8):
                eng = evac_engine()
                o = dst[:, kc, dst_off:dst_off + 128]
                i_ = pt_[:, kc * 128:(kc + 1) * 128]
                sc_ = A1[:, which, kc, ci:ci + 1]
                bi_ = SH[:, which, kc, ci:ci + 1]
                if eng == 'act':
                    S.op('act', lambda e, o=o, i_=i_, sc_=sc_, bi_=bi_: e.activation(out=o, in_=i_, func=AF.Identity, scale=sc_, bias=bi_),
                         reads=[pr, rMod], writes=[dst_res])
                else:
                    S.op('dve', lambda e, o=o, i_=i_, sc_=sc_, bi_=bi_: e.tensor_scalar(out=o, in0=i_, scalar1=sc_, scalar2=bi_,
                                                                                        op0=ALU.mult, op1=ALU.add),
                         reads=[pr, rMod], writes=[dst_res])
            pb.release(b)

        def residual_epilogue(t, banks, G, rG):
            ss, rss = scr.next()
            jk, rjk = ybf.next()
            for c in range(2):
                S.op('act', lambda e, c=c: e.activation(out=jk[:, c * 512:(c + 1) * 512], in_=pf.t[banks[c]][:], func=AF.Square,
                                                        accum_out=ss[:, c:c + 1]),
                     reads=[pf.res[banks[c]]], writes=[rjk, rss])
            S.op('dve', lambda e: e.tensor_tensor(out=ss[:, 2:3], in0=ss[:, 0:1], in1=ss[:, 1:2], op=ALU.add), reads=[rss], writes=[rss])
            rs, rrs = rstd_from(ss[:, 2:3], D, [rss])
            for c in range(2):
                tm, rtm = rtmp.next()
                S.op('dve', lambda e, c=c, tm=tm: e.scalar_tensor_tensor(out=tm[:], in0=pf.t[banks[c]][:], scalar=rs[:, 0:1],
                                                                         in1=G[:, c * 512:(c + 1) * 512], op0=ALU.mult, op1=ALU.mult),
                     reads=[pf.res[banks[c]], rrs, rG], writes=[rtm])
                S.op('dve', lambda e, c=c, tm=tm: e.tensor_tensor(out=X[:, t, c * 512:(c + 1) * 512], in0=X[:, t, c * 512:(c + 1) * 512],
                                                                  in1=tm[:], op=ALU.add),
                     reads=[rtm, Xr[t]], writes=[Xr[t]])

        def gate_rows(l, chunk, gidx, wbuf, rw, G, rG, which_list, Gtmp, rGtmp):
            for h in range(2):
                S.dma('pool', wbuf[:, :, h * 512:(h + 1) * 512], wview(wmod_d[l], chunk * D + h * 512, chunk * D + (h + 1) * 512), writes=[rw])
            for i in which_list:
                S.dma('sp', G[i][:], bmod_d[l:l + 1, chunk * D:(chunk + 1) * D].partition_broadcast(128), writes=[rG[i]])
            ngb, rngb = Gtmp, rGtmp
            S.dma('sp', ngb[:], ng_d[l, gidx:gidx + 1, :].partition_broadcast(128), writes=[rngb])
            for i in which_list:
                for c in range(2):
                    b = pf.alloc()
                    for kc in range(8):
                        S.op('pe', lambda e, kc=kc, c=c, b=b, i=i: e.matmul(pf.t[b][:], lhsT=sRep[:, i, kc, :], rhs=wbuf[:, kc, c * 512:(c + 1) * 512],
                                                                       start=(kc == 0), stop=(kc == 7)),
                             reads=[rC, rw], writes=[pf.res[b]])
                    S.op('dve', lambda e, c=c, b=b, i=i: e.tensor_tensor(out=G[i][:, c * 512:(c + 1) * 512], in0=pf.t[b][:],
                                                                    in1=G[i][:, c * 512:(c + 1) * 512], op=ALU.add),
                         reads=[pf.res[b], rG[i]], writes=[rG[i]])
                    pf.release(b)
                S.op('dve', lambda e, i=i: e.tensor_tensor(out=G[i][:], in0=G[i][:], in1=ngb[:], op=ALU.mult), reads=[rG[i], rngb], writes=[rG[i]])

        def run_layer(l):
            last = (l == NLAYERS - 1)
            tiles_upd = list(range(2, NT)) if last else list(range(NT))
            which_list = [0] if last else [0, 1]
            S.barrier()
            with ExitStack() as ph:
                wm = Ring([sb(ph, "wm%d" % i, [128, 8, D], BF16) for i in range(2)])
                for ch in range(0 if os.environ.get("KDBG_SKIPPROJ") else 6):
                    w_, rw_ = wm.next()
                    for h in range(2):
                        S.dma('pool', w_[:, :, h * 512:(h + 1) * 512], wview(wmod_d[l], ch * D + h * 512, ch * D + (h + 1) * 512), writes=[rw_])
                    b = pf.alloc()
                    for fc in range(8):
                        for kc in range(8):
                            S.op('pe', lambda e, fc=fc, kc=kc, w_=w_, b=b: e.matmul(pf.t[b][:, fc * 2:fc * 2 + 2], lhsT=w_[:, kc, fc * 128:(fc + 1) * 128],
                                                                                rhs=sT[:, kc, :], start=(kc == 0), stop=(kc == 7)),
                                 reads=[rw_, rC], writes=[pf.res[b]])
                    S.op('dve', lambda e, ch=ch, b=b: e.tensor_tensor(
                        out=modcol[:, ch, :, :], in0=pf.t[b][:, 0:16].rearrange("p (f i) -> p f i", i=2),
                        in1=bmodcol[:, l, ch * 8:(ch + 1) * 8].unsqueeze(2).to_broadcast([128, 8, 2]), op=ALU.add),
                        reads=[pf.res[b], rC], writes=[rMod])
                    pf.release(b)
                for which, (shc, scc, gi) in enumerate([(0, 1, 0), (3, 4, 2)]):
                    S.op('dve', lambda e, which=which, scc=scc, gi=gi: e.scalar_tensor_tensor(
                        out=A1[:, which, :, :], in0=modcol[:, scc, :, :], scalar=1.0,
                        in1=ngcol[:, l, gi, :].unsqueeze(2).to_broadcast([128, 8, 2]), op0=ALU.add, op1=ALU.mult),
                        reads=[rMod, rC], writes=[rMod])
                    S.op('dve', lambda e, which=which, shc=shc: e.tensor_copy(out=SH[:, which, :, :], in_=modcol[:, shc, :, :]),
                         reads=[rMod], writes=[rMod])
            S.barrier()
            with ExitStack() as ph:
                qkT = sb(ph, "qkT", [128, 14, NTOK], BF16)
                qkr = [[Res() for _ in range(NT)] for _ in range(14)]
                phv = ph.enter_context(ExitStack())
                V = sb(phv, "V", [128, NT, 640], BF16)
                Vr = [Res() for _ in range(NT)]
                with ExitStack() as ph2:
                    win = sb(ph2, "win", [128, 8, 1024], BF16)
                    rwin = Res()
                    hTr = Ring([sb(ph2, "hT%d" % i, [128, 8, 128], BF16) for i in range(2)])
                    csr = Ring([sb(ph2, "cs%d" % i, [128, 128], F32) for i in range(2)])
                    cur = {}
                    gq = sb(ph2, "gq", [128, 2, 64], F32)
                    if l == 0:
                        S.dma('sp', gq[:].rearrange("p a d -> p (a d)"), qkg_d.partition_broadcast(128), writes=[rC])
                        S.op('dve', lambda e: e.tensor_scalar_mul(out=gq[:, 0, :], in0=gq[:, 0, :], scalar1=0.125),
                             reads=[rC], writes=[rC])
                    stg = Ring([sb(ph2, "stg%d" % i, [128, 512], BF16) for i in range(3)])
                    f32t = Ring([sb(ph2, "f32t%d" % i, [128, 512], F32) for i in range(3)])

                    def transpose_store(st, rst, npt, slot0, t, src_cols=None):
                        b = pb.alloc()
                        for j in range(npt):
                            c0 = j * 128 if src_cols is None else src_cols[j]
                            S.op('pe', lambda e, j=j, c0=c0, b=b: e.transpose(pb.t[b][:, j * 128:(j + 1) * 128], st[:, c0:c0 + 128], ident[:]),
                                 reads=[rst, rC], writes=[pb.res[b]])
                        eng = evac_engine()
                        o = qkT[:, slot0:slot0 + npt, t * 128:(t + 1) * 128]
                        i_ = pb.t[b][:, 0:npt * 128].rearrange("p (s n) -> p s n", n=128)
                        wr = [qkr[s][t] for s in range(slot0, slot0 + npt)]
                        if eng == 'act':
                            S.op('act', lambda e: e.activation(out=o, in_=i_, func=AF.Copy), reads=[pb.res[b]], writes=wr)
                        else:
                            S.op('dve', lambda e: e.tensor_copy(out=o, in_=i_), reads=[pb.res[b]], writes=wr)
                        pb.release(b)

                    def rope_to(st, rst, src, src_reads, nh, t, scale, dst_cols0=0):
                        n = nh * 64
                        cs_, rcs_ = cur["cs"]
                        t1, r1 = f32t.next()
                        t2, r2 = f32t.next()
                        s4 = src.rearrange("p (h d) -> p h d", d=64)
                        S.op('dve', lambda e: e.scalar_tensor_tensor(
                            out=t1[:, 0:n].rearrange("p (h d) -> p h d", d=64), in0=s4, scalar=scale,
                            in1=cs_[:, 0:64].unsqueeze(1).to_broadcast([128, nh, 64]), op0=ALU.mult, op1=ALU.mult),
                            reads=src_reads + [rcs_], writes=[r1])
                        s5 = src.rearrange("p (h a g f) -> p h a g f", a=2, g=2, f=16)
                        o5 = t2[:, 0:n].rearrange("p (h a g f) -> p h a g f", a=2, g=2, f=16)
                        sn5 = cs_[:, 64:128].rearrange("p (a g f) -> p a g f", a=2, g=2)
                        for g in range(2):
                            for a in range(2):
                                S.op('dve', lambda e, g=g, a=a: e.scalar_tensor_tensor(
                                    out=o5[:, :, a, g, :], in0=s5[:, :, a, 1 - g, :], scalar=scale,
                                    in1=sn5[:, a, g, :].unsqueeze(1).to_broadcast([128, nh, 16]), op0=ALU.mult, op1=ALU.mult),
                                    reads=src_reads + [rcs_], writes=[r2])
                        S.op('pool', lambda e: e.tensor_tensor(out=st[:, dst_cols0:dst_cols0 + n], in0=t1[:, 0:n], in1=t2[:, 0:n], op=ALU.add),
                             reads=[r1, r2], writes=[rst])

                    def headnorm(bk, c0, nh, gi):
                        n = nh * 64
                        sq, rsq = f32t.next()
                        S.op('act', lambda e: e.activation(out=sq[:, 0:n], in_=pf.t[bk][:, c0:c0 + n], func=AF.Square),
                             reads=[pf.res[bk]], writes=[rsq])
                        ssh, rssh = scr.next()
                        S.op('dve', lambda e: e.tensor_reduce(out=ssh[:, 0:nh], in_=sq[:, 0:n].rearrange("p (h d) -> p h d", d=64), axis=AX.X, op=ALU.add),
                             reads=[rsq], writes=[rssh])
                        rs, rrs = rstd_from(ssh[:, 0:nh], 64, [rssh])
                        qn, rqn = f32t.next()
                        S.op('dve', lambda e: e.tensor_tensor(out=qn[:, 0:n].rearrange("p (h d) -> p h d", d=64),
                                                              in0=pf.t[bk][:, c0:c0 + n].rearrange("p (h d) -> p h d", d=64),
                                                              in1=rs.unsqueeze(2).to_broadcast([128, nh, 64]), op=ALU.mult),
                             reads=[pf.res[bk], rrs], writes=[rqn])
                        S.op('pool', lambda e: e.tensor_tensor(out=qn[:, 0:n].rearrange("p (h d) -> p h d", d=64),
                                                               in0=qn[:, 0:n].rearrange("p (h d) -> p h d", d=64),
                                                               in1=gq[:, gi, :].unsqueeze(1).to_broadcast([128, nh, 64]), op=ALU.mult),
                             reads=[rqn, rC], writes=[rqn])
                        return qn, rqn

                    blocks = [(0, 512), (512, 768), (768, 1280), (1280, 1792), (1792, 2304)]
                    for pblocks in ([] if os.environ.get("KDBG_SKIPPROJ") else [[0, 1], [2, 3], [4]]):
                      pc0 = blocks[pblocks[0]][0]
                      pc1 = blocks[pblocks[-1]][1]
                      for cb in range((pc1 - pc0) // 256):
                          S.dma('pool', win[:, :, cb * 256:(cb + 1) * 256], wview(win_d[l], pc0 + cb * 256, pc0 + (cb + 1) * 256), writes=[rwin])
                      need_rope = (pblocks[0] == 0) or (l == 0 and pblocks[0] == 2)
                      for t in range(NT):
                        hT, rhT = hTr.next()
                        norm_mod_transpose(t, 0, hT, rhT, 0)
                        is_x = t >= 2
                        if is_x and need_rope:
                            cs_, rcs_ = csr.next()
                            S.dma('sp', cs_[:, 0:64], cos_d[:, t - 2, :], writes=[rcs_])
                            S.dma('sp', cs_[:, 64:128], sin_d[:, t - 2, :], writes=[rcs_])
                            cur["cs"] = (cs_, rcs_)
                        for bi in pblocks:
                            c0, c1 = blocks[bi]
                            n = c1 - c0
                            bk = pf.alloc()
                            for kc in range(8):
                                S.op('pe', lambda e, kc=kc, bk=bk: e.matmul(pf.t[bk][:, 0:n], lhsT=hT[:, kc, :], rhs=win[:, kc, c0 - pc0:c1 - pc0],
                                                                        start=(kc == 0), stop=(kc == 7)),
                                     reads=[rhT, rwin], writes=[pf.res[bk]])
                            pr = pf.res[bk]
                            ps = pf.t[bk]
                            if bi == 0:
                                st, rst = stg.next()
                                if l == 0:
                                    qn, rqn = headnorm(bk, 0, 8, 0)
                                    if is_x:
                                        rope_to(st, rst, qn[:, 0:512], [rqn], 8, t, 1.0)
                                    else:
                                        S.op('act', lambda e: e.activation(out=st[:, 0:512], in_=qn[:, 0:512], func=AF.Copy), reads=[rqn], writes=[rst])
                                else:
                                    if is_x:
                                        rope_to(st, rst, ps[:, 0:512], [pr], 8, t, 0.125)
                                    else:
                                        S.op('act', lambda e: e.activation(out=st[:, 0:512], in_=ps[:, 0:512], func=AF.Identity, scale=0.125), reads=[pr], writes=[rst])
                                transpose_store(st, rst, 4, 0, t)
                            elif bi == 1:
                                st, rst = stg.next()
                                if l == 0:
                                    kn, rkn = headnorm(bk, 0, 2, 1)
                                    src, srcr = kn[:, 0:128], [rkn]
                                else:
                                    src, srcr = ps[:, 0:128], [pr]
                                if is_x:
                                    rope_to(st, rst, src, srcr, 2, t, 1.0, dst_cols0=0)
                                else:
                                    S.op('act', lambda e: e.activation(out=st[:, 0:128], in_=src, func=AF.Copy), reads=srcr, writes=[rst])
                                S.op('pool', lambda e: e.tensor_copy(
                                    out=st[:, 128:384].rearrange("p (g c d) -> p g c d", g=2, c=2),
                                    in_=st[:, 0:128].rearrange("p (g d) -> p g d", g=2).unsqueeze(2).to_broadcast([128, 2, 2, 64])),
                                    reads=[rst], writes=[rst])
                                transpose_store(st, rst, 2, 8, t, src_cols=[128, 256])
                                S.op('act', lambda e: e.activation(out=V[:, t, 0:128], in_=ps[:, 128:256], func=AF.Copy), reads=[pr], writes=[Vr[t]])
                            elif bi == 2:
                                st, rst = stg.next()
                                if l == 0 and is_x:
                                    rope_to(st, rst, ps[:, 0:512], [pr], 8, t, 0.125)
                                else:
                                    S.op('act', lambda e: e.activation(out=st[:, 0:512], in_=ps[:, 0:512], func=AF.Identity, scale=0.125), reads=[pr], writes=[rst])
                                transpose_store(st, rst, 4, 4, t)
                            elif bi == 3:
                                st, rst = stg.next()
                                if l == 0 and is_x:
                                    rope_to(st, rst, ps[:, 0:512], [pr], 8, t, 1.0)
                                else:
                                    S.op('act', lambda e: e.activation(out=st[:, 0:512], in_=ps[:, 0:512], func=AF.Copy), reads=[pr], writes=[rst])
                                transpose_store(st, rst, 4, 10, t)
                            else:
                                S.op('dve', lambda e: e.tensor_copy(out=V[:, t, 128:640], in_=ps[:, 0:512]), reads=[pr], writes=[Vr[t]])
                            pf.release(bk)
                S.barrier()
                if debug_stop == (l, 'proj'):
                    return True
                with ExitStack() as ph2:
                    Pt = Ring([sb(ph2, "Pt%d" % i, [128, 512], BF16) for i in range(16 if l == 0 else 9)])
                    f32a = Ring([sb(ph2, "f32a%d" % i, [128, 512], F32) for i in range(6 if l == 0 else 4)])
                    if l == 0:
                        lamp = sb(ph2, "lamp", [128, 4, 64], F32)
                        rl = Res()
                        S.dma('sp', lamp[:].rearrange("p a d -> p (a d)"), lamp_d.partition_broadcast(128), writes=[rl])
                        lt, rlt = scr.next()
                        for j in range(2):
                            pr_, rpr_ = f32a.next()
                            S.op('dve', lambda e, j=j, pr_=pr_: e.tensor_tensor(out=pr_[:, 0:64], in0=lamp[:, 2 * j, :], in1=lamp[:, 2 * j + 1, :], op=ALU.mult),
                                 reads=[rl], writes=[rpr_])
                            S.op('dve', lambda e, j=j, pr_=pr_: e.tensor_reduce(out=lt[:, j:j + 1], in_=pr_[:, 0:64], axis=AX.X, op=ALU.add),
                                 reads=[rpr_], writes=[rlt])
                        S.op('act', lambda e: e.activation(out=lt[:, 2:4], in_=lt[:, 0:2], func=AF.Exp), reads=[rlt], writes=[rlt])
                        S.op('dve', lambda e: e.tensor_tensor(out=lt[:, 4:5], in0=lt[:, 3:4], in1=lt[:, 2:3], op=ALU.subtract), reads=[rlt], writes=[rlt])
                        S.op('dve', lambda e: e.tensor_scalar_add(out=small[:, 0:1], in0=lt[:, 4:5], scalar1=-lambda_init(0)),
                             reads=[rlt], writes=[rC])
                        S.dma('sp', small[:, 1:2], subln_d, writes=[rC])
                        S.op('dve', lambda e: e.tensor_scalar_mul(out=small[:, 2:3], in0=small[:, 1:2], scalar1=1.0 - lambda_init(0)),
                             reads=[rC], writes=[rC])
                        neglam = small[:, 0:1]
                        sgl = small[:, 2:3]
                        qblocks = [(0, 256, [0, 1], [0, 1])] + [(256 + 512 * i, 512, list(range(NT)), [2 + 4 * i + j for j in range(4)]) for i in range(4)]
                        for (tok0, n, kts, qtiles) in qblocks:
                            for pt in range(4):
                                bo = pf.alloc()
                                bd = pf.alloc()
                                for half in range(2):
                                    h = 2 * pt + half
                                    kv = h // 4
                                    hp = slice(half * 64, (half + 1) * 64)
                                    def qk_a(kt, hp=hp, kv=kv, pt=pt, half=half):
                                        bs = pf.alloc()
                                        S.op('pe', lambda e: e.matmul(pf.t[bs][:, 0:n], lhsT=qkT[hp, 8 + kv, kt * 128:(kt + 1) * 128],
                                                                      rhs=qkT[hp, pt, tok0:tok0 + n], start=True, stop=True, tile_position=(half * 64, 0)),
                                             reads=[qkr[8 + kv][kt]] + [qkr[pt][q] for q in qtiles], writes=[pf.res[bs]])
                                        p_, rp_ = Pt.next()
                                        S.op('act', lambda e: e.activation(out=p_[:, 0:n], in_=pf.t[bs][:, 0:n], func=AF.Exp),
                                             reads=[pf.res[bs]], writes=[rp_])
                                        pf.release(bs)
                                        return p_, rp_

                                    def pv_a(ki, kt, p_, rp_, hp=hp, kv=kv, half=half):
                                        S.op('pe', lambda e: e.matmul(pf.t[bo][hp, 0:n], lhsT=V[:, kt, kv * 64:(kv + 1) * 64], rhs=p_[:, 0:n],
                                                                      start=(ki == 0), stop=(ki == len(kts) - 1), tile_position=(0, half * 64)),
                                             reads=[Vr[kt], rp_], writes=[pf.res[bo]])
                                        S.op('pe', lambda e: e.matmul(pf.t[bd][hp, 0:n], lhsT=ones[:, 0:64], rhs=p_[:, 0:n],
                                                                      start=(ki == 0), stop=(ki == len(kts) - 1), tile_position=(0, half * 64)),
                                             reads=[rC, rp_], writes=[pf.res[bd]])

                                    pipeline(kts, qk_a, pv_a)
                                rr, rrr = f32a.next()
                                S.op('dve', lambda e: e.reciprocal(out=rr[:, 0:n], in_=pf.t[bd][:, 0:n]), reads=[pf.res[bd]], writes=[rrr])
                                S.op('dve', lambda e: e.tensor_tensor(out=qkT[:, pt, tok0:tok0 + n], in0=pf.t[bo][:, 0:n], in1=rr[:, 0:n], op=ALU.mult),
                                     reads=[pf.res[bo], rrr], writes=[qkr[pt][q] for q in qtiles])
                                pf.release(bo)
                                pf.release(bd)
                            for hb in range(4):
                                bo = [pf.alloc(), pf.alloc()]
                                bd = [pf.alloc(), pf.alloc()]
                                for j in range(2):
                                    hp = slice(j * 64, (j + 1) * 64)
                                    def qk_b(kt, hp=hp, j=j, hb=hb):
                                        bs = pf.alloc()
                                        S.op('pe', lambda e: e.matmul(pf.t[bs][:, 0:n], lhsT=qkT[hp, 10 + hb, kt * 128:(kt + 1) * 128],
                                                                      rhs=qkT[hp, 4 + hb, tok0:tok0 + n], start=True, stop=True, tile_position=(j * 64, 0)),
                                             reads=[qkr[10 + hb][kt]] + [qkr[4 + hb][q] for q in qtiles], writes=[pf.res[bs]])
                                        p_, rp_ = Pt.next()
                                        S.op('act', lambda e: e.activation(out=p_[:, 0:n], in_=pf.t[bs][:, 0:n], func=AF.Exp),
                                             reads=[pf.res[bs]], writes=[rp_])
                                        pf.release(bs)
                                        return p_, rp_

                                    def pv_b(ki, kt, p_, rp_, j=j, hb=hb):
                                        S.op('pe', lambda e: e.matmul(pf.t[bo[j]][:, 0:n], lhsT=V[:, kt, 128 + hb * 128:128 + (hb + 1) * 128], rhs=p_[:, 0:n],
                                                                      start=(ki == 0), stop=(ki == len(kts) - 1)),
                                             reads=[Vr[kt], rp_], writes=[pf.res[bo[j]]])
                                        S.op('pe', lambda e: e.matmul(pf.t[bd[j]][:, 0:n], lhsT=ones[:, :], rhs=p_[:, 0:n],
                                                                      start=(ki == 0), stop=(ki == len(kts) - 1)),
                                             reads=[rC, rp_], writes=[pf.res[bd[j]]])

                                    pipeline(kts, qk_b, pv_b)
                                oo = []
                                for j in range(2):
                                    rr, rrr = f32a.next()
                                    S.op('dve', lambda e, j=j, rr=rr: e.reciprocal(out=rr[:, 0:n], in_=pf.t[bd[j]][:, 0:n]), reads=[pf.res[bd[j]]], writes=[rrr])
                                    o_, ro_ = f32a.next()
                                    S.op('dve', lambda e, j=j, rr=rr, o_=o_: e.tensor_tensor(out=o_[:, 0:n], in0=pf.t[bo[j]][:, 0:n], in1=rr[:, 0:n], op=ALU.mult),
                                         reads=[pf.res[bo[j]], rrr], writes=[ro_])
                                    oo.append((o_, ro_))
                                for j in range(2):
                                    pf.release(bo[j])
                                    pf.release(bd[j])
                                ob, rob = f32a.next()
                                S.op('dve', lambda e: e.scalar_tensor_tensor(out=ob[:, 0:n], in0=oo[1][0][:, 0:n], scalar=neglam, in1=oo[0][0][:, 0:n],
                                                                              op0=ALU.mult, op1=ALU.add),
                                     reads=[oo[0][1], oo[1][1], rC], writes=[rob])
                                sq, rsq = Pt.next()
                                S.op('pool', lambda e: e.tensor_tensor(out=sq[:, 0:n], in0=ob[:, 0:n], in1=ob[:, 0:n], op=ALU.mult), reads=[rob], writes=[rsq])
                                bq = pf.alloc()
                                S.op('pe', lambda e: e.matmul(pf.t[bq][:, 0:n], lhsT=ones[:, :], rhs=sq[:, 0:n], start=True, stop=True),
                                     reads=[rC, rsq], writes=[pf.res[bq]])
                                rs_, rrs_ = f32a.next()
                                S.op('dve', lambda e: e.tensor_scalar(out=rs_[:, 0:n], in0=pf.t[bq][:, 0:n], scalar1=1.0 / 128, scalar2=EPS, op0=ALU.mult, op1=ALU.add),
                                     reads=[pf.res[bq]], writes=[rrs_])
                                pf.release(bq)
                                S.op('act', lambda e: e.activation(out=rs_[:, 0:n], in_=rs_[:, 0:n], func=AF.Sqrt), reads=[rrs_], writes=[rrs_])
                                S.op('dve', lambda e: e.reciprocal(out=rs_[:, 0:n], in_=rs_[:, 0:n]), reads=[rrs_], writes=[rrs_])
                                S.op('dve', lambda e: e.scalar_tensor_tensor(out=qkT[:, 4 + hb, tok0:tok0 + n], in0=ob[:, 0:n], scalar=sgl, in1=rs_[:, 0:n],
                                                                             op0=ALU.mult, op1=ALU.mult),
                                     reads=[rob, rrs_, rC], writes=[qkr[4 + hb][q] for q in qtiles])
                    else:
                        tbl = sb(ph2, "tbl", [128, 4, 16, 64], F32)
                        rtbl = Res()
                        qzr = Ring([sb(ph2, "qz%d" % i, [128, 2, 128], BF16) for i in range(4)])
                        for qz_ in qzr.t:
                            S.op('pool', lambda e, qz_=qz_: e.memset(qz_[:], 0.0), writes=[rC])

                        def make_qz(slot, t):
                            qz_, rqz_ = qzr.next()
                            S.op('pool', lambda e: e.tensor_copy(out=qz_[0:64, 0, :], in_=qkT[0:64, slot, t * 128:(t + 1) * 128]),
                                 reads=[qkr[slot][t], rC], writes=[rqz_])
                            S.op('dve', lambda e: e.tensor_copy(out=qz_[64:128, 1, :], in_=qkT[64:128, slot, t * 128:(t + 1) * 128]),
                                 reads=[qkr[slot][t], rC], writes=[rqz_])
                            return qz_, rqz_
                        mprev = sb(ph2, "mprev", [128, 128], BF16)
                        mnext = sb(ph2, "mnext", [128, 128], BF16)
                        S.dma('pool', mprev[:], mprev_d, writes=[rC])
                        S.dma('pool', mnext[:], mnext_d, writes=[rC])
                        esink = sb(ph2, "esink", [128, 8], F32)
                        S.dma('sp', esink[:], sink_d.partition_broadcast(128), writes=[rC])
                        S.op('act', lambda e: e.activation(out=esink[:], in_=esink[:], func=AF.Exp), reads=[rC], writes=[rC])
                        for i in range(int(os.environ.get("KDBG_NI", "16"))):
                            t = i + 2
                            tok0 = t * 128
                            for g in range(2):
                                klist = [(0, None), (1, None)]
                                if i >= 1:
                                    klist.append((t - 1, mprev))
                                klist.append((t, None))
                                if i <= 14:
                                    klist.append((t + 1, mnext))
                                pts = []
                                qzs = [make_qz(2 * g + j, t) for j in range(2)]
                                for (kt, msk) in klist:
                                    bs = pf.alloc()
                                    for hq in range(4):
                                        half = hq % 2
                                        qz_, rqz_ = qzs[hq // 2]
                                        S.op('pe', lambda e, bs=bs, kt=kt, hq=hq, half=half, qz_=qz_: e.matmul(
                                            pf.t[bs][:, hq * 128:(hq + 1) * 128], lhsT=qkT[:, 8 + g, kt * 128:(kt + 1) * 128],
                                            rhs=qz_[:, half, :], start=True, stop=True),
                                            reads=[qkr[8 + g][kt], rqz_], writes=[pf.res[bs]])
                                    p_, rp_ = Pt.next()
                                    S.op('act', lambda e, bs=bs, p_=p_: e.activation(out=p_[:], in_=pf.t[bs][:], func=AF.Exp), reads=[pf.res[bs]], writes=[rp_])
                                    pf.release(bs)
                                    if msk is not None:
                                        S.op('pool', lambda e, p_=p_, msk=msk: e.tensor_tensor(
                                            out=p_[:].rearrange("p (h q) -> p h q", h=4), in0=p_[:].rearrange("p (h q) -> p h q", h=4),
                                            in1=msk[:].unsqueeze(1).to_broadcast([128, 4, 128]), op=ALU.mult), reads=[rp_, rC], writes=[rp_])
                                    pts.append((kt, p_, rp_))
                                bo = pf.alloc()
                                bd = pf.alloc()
                                for hq in range(4):
                                    ptl = hq // 2
                                    half = hq % 2
                                    hp = slice(half * 64, (half + 1) * 64)
                                    for ki, (kt, p_, rp_) in enumerate(pts):
                                        S.op('pe', lambda e, kt=kt, p_=p_, ki=ki, hq=hq, ptl=ptl, hp=hp, half=half: e.matmul(
                                            pf.t[bo][hp, ptl * 128:(ptl + 1) * 128], lhsT=V[:, kt, g * 64:(g + 1) * 64], rhs=p_[:, hq * 128:(hq + 1) * 128],
                                            start=(ki == 0), stop=(ki == len(pts) - 1), tile_position=(0, half * 64)),
                                            reads=[Vr[kt], rp_], writes=[pf.res[bo]])
                                        S.op('pe', lambda e, p_=p_, ki=ki, hq=hq, ptl=ptl, hp=hp, half=half: e.matmul(
                                            pf.t[bd][hp, ptl * 128:(ptl + 1) * 128], lhsT=ones[:, 0:64], rhs=p_[:, hq * 128:(hq + 1) * 128],
                                            start=(ki == 0), stop=(ki == len(pts) - 1), tile_position=(0, half * 64)),
                                            reads=[rC, rp_], writes=[pf.res[bd]])
                                rr, rrr = f32a.next()
                                for hq in range(4):
                                    ptl = hq // 2
                                    half = hq % 2
                                    hp = slice(half * 64, (half + 1) * 64)
                                    h = 4 * g + hq
                                    S.op('dve', lambda e, hp=hp, ptl=ptl, h=h: e.tensor_scalar_add(
                                        out=rr[hp, ptl * 128:(ptl + 1) * 128], in0=pf.t[bd][hp, ptl * 128:(ptl + 1) * 128],
                                        scalar1=esink[hp, h:h + 1]), reads=[pf.res[bd], rC], writes=[rrr])
                                S.op('dve', lambda e: e.reciprocal(out=rr[:, 0:256], in_=rr[:, 0:256]), reads=[rrr], writes=[rrr])
                                S.op('dve', lambda e: e.tensor_tensor(out=qkT[:, 2 * g:2 * g + 2, tok0:tok0 + 128],
                                                                      in0=pf.t[bo][:, 0:256].rearrange("p (s q) -> p s q", s=2),
                                                                      in1=rr[:, 0:256].rearrange("p (s q) -> p s q", s=2), op=ALU.mult),
                                     reads=[pf.res[bo], rrr], writes=[qkr[2 * g][t], qkr[2 * g + 1][t]])
                                pf.release(bo)
                                pf.release(bd)
                        if debug_stop == (l, 'c'):
                            return True
                        for hg in range(2):
                            S.dma('sp', tbl[:], tbl_d[:, hg * 4:(hg + 1) * 4], writes=[rtbl])
                            for i in range(16):
                                t = i + 2
                                tok0 = t * 128
                                r0, r1 = 2 * i, 2 * i + 1
                                rs0 = min(max(r0 - 4, 0), 24)
                                rs1 = min(max(r1 - 4, 0), 24)
                                kx_lo = rs0 // 2
                                kx_hi = (rs1 + 8 + 1) // 2
                                klist = [(0, None), (1, None)]
                                for kx in range(kx_lo, kx_hi):
                                    jj = {}
                                    for kr in range(2):
                                        for qr in range(2):
                                            krow = 2 * kx + kr
                                            qrow = 2 * i + qr
                                            rs_q = min(max(qrow - 4, 0), 24)
                                            if rs_q <= krow < rs_q + 8:
                                                jj[(kr, qr)] = krow - qrow + 7
                                            else:
                                                jj[(kr, qr)] = 15
                                    klist.append((kx + 2, jj))
                                pts = []
                                qzs = [make_qz(4 + hg * 2 + j, t) for j in range(2)]
                                for (kt, jj) in klist:
                                    bs = pf.alloc()
                                    for hq in range(4):
                                        h = hg * 4 + hq
                                        half = h % 2
                                        qz_, rqz_ = qzs[hq // 2]
                                        S.op('pe', lambda e, bs=bs, kt=kt, hq=hq, h=h, half=half, qz_=qz_: e.matmul(
                                            pf.t[bs][:, hq * 128:(hq + 1) * 128], lhsT=qkT[:, 10 + h // 2, kt * 128:(kt + 1) * 128],
                                            rhs=qz_[:, half, :], start=True, stop=True),
                                            reads=[qkr[10 + h // 2][kt], rqz_], writes=[pf.res[bs]])
                                    p_, rp_ = Pt.next()
                                    if jj is None:
                                        S.op('act', lambda e, bs=bs, p_=p_: e.activation(out=p_[:], in_=pf.t[bs][:], func=AF.Exp), reads=[pf.res[bs]], writes=[rp_])
                                    else:
                                        sbf, rsbf = f32a.next()
                                        for kr in range(2):
                                            for qr in range(2):
                                                kp = slice(kr * 64, (kr + 1) * 64)
                                                S.op('dve', lambda e, bs=bs, kp=kp, qr=qr, kr=kr, jj=jj, sbf=sbf: e.tensor_tensor(
                                                    out=sbf[kp, :].rearrange("p (h q) -> p h q", h=4)[:, :, qr * 64:(qr + 1) * 64],
                                                    in0=pf.t[bs][kp, :].rearrange("p (h q) -> p h q", h=4)[:, :, qr * 64:(qr + 1) * 64],
                                                    in1=tbl[kp, :, jj[(kr, qr)], :], op=ALU.add),
                                                    reads=[pf.res[bs], rtbl], writes=[rsbf])
                                        S.op('act', lambda e, sbf=sbf, p_=p_: e.activation(out=p_[:], in_=sbf[:], func=AF.Exp), reads=[rsbf], writes=[rp_])
                                    pf.release(bs)
                                    pts.append((kt, p_, rp_))
                                bo = pf.alloc()
                                bd = pf.alloc()
                                for hq in range(4):
                                    h = hg * 4 + hq
                                    ptl = hq // 2
                                    half = hq % 2
                                    hp = slice(half * 64, (half + 1) * 64)
                                    for ki, (kt, p_, rp_) in enumerate(pts):
                                        S.op('pe', lambda e, kt=kt, p_=p_, ki=ki, hq=hq, ptl=ptl, hp=hp, half=half, h=h: e.matmul(
                                            pf.t[bo][hp, ptl * 128:(ptl + 1) * 128], lhsT=V[:, kt, 128 + h * 64:128 + (h + 1) * 64], rhs=p_[:, hq * 128:(hq + 1) * 128],
                                            start=(ki == 0), stop=(ki == len(pts) - 1), tile_position=(0, half * 64)),
                                            reads=[Vr[kt], rp_], writes=[pf.res[bo]])
                                        S.op('pe', lambda e, p_=p_, ki=ki, hq=hq, ptl=ptl, hp=hp, half=half: e.matmul(
                                            pf.t[bd][hp, ptl * 128:(ptl + 1) * 128], lhsT=ones[:, 0:64], rhs=p_[:, hq * 128:(hq + 1) * 128],
                                            start=(ki == 0), stop=(ki == len(pts) - 1), tile_position=(0, half * 64)),
                                            reads=[rC, rp_], writes=[pf.res[bd]])
                                rr, rrr = f32a.next()
                                S.op('dve', lambda e: e.reciprocal(out=rr[:, 0:256], in_=pf.t[bd][:, 0:256]), reads=[pf.res[bd]], writes=[rrr])
                                s0 = 4 + hg * 2
                                S.op('dve', lambda e: e.tensor_tensor(out=qkT[:, s0:s0 + 2, tok0:tok0 + 128],
                                                                      in0=pf.t[bo][:, 0:256].rearrange("p (s q) -> p s q", s=2),
                                                                      in1=rr[:, 0:256].rearrange("p (s q) -> p s q", s=2), op=ALU.mult),
                                     reads=[pf.res[bo], rrr], writes=[qkr[s0][t], qkr[s0 + 1][t]])
                                pf.release(bo)
                                pf.release(bd)
                S.barrier()
                phv.close()
                with ExitStack() as ph2:
                    wo = sb(ph2, "wo", [128, 8, D], BF16)
                    rwo = Res()
                    G = [sb(ph2, "G%d" % i, [128, D], F32) for i in which_list]
                    rG = [Res() for _ in which_list]
                    Gtmp = sb(ph2, "Gtmp", [128, D], F32)
                    rGtmp = Res()
                    gate_rows(l, 2, 1, wo, rwo, G, rG, which_list, Gtmp, rGtmp)
                    for h in range(2):
                        S.dma('pool', wo[:, :, h * 512:(h + 1) * 512], wview(wout_d[l], h * 512, (h + 1) * 512), writes=[rwo])
                    for t in tiles_upd:
                        ci = 1 if t < 2 else 0
                        banks = [pf.alloc(), pf.alloc()]
                        for c in range(2):
                            for s in range(8):
                                S.op('pe', lambda e, c=c, s=s: e.matmul(pf.t[banks[c]][:], lhsT=qkT[:, s, t * 128:(t + 1) * 128],
                                                                      rhs=wo[:, s, c * 512:(c + 1) * 512], start=(s == 0), stop=(s == 7)),
                                     reads=[qkr[s][t], rwo], writes=[pf.res[banks[c]]])
                        residual_epilogue(t, banks, G[ci], rG[ci])
                        pf.release(banks[0])
                        pf.release(banks[1])
            S.barrier()
            if debug_stop == (l, 'attn'):
                return True
            with ExitStack() as ph:
                W2 = sb(ph, "W2", [128, 32, D], BF16)
                rW2 = Res()
                G = [sb(ph, "G2_%d" % i, [128, D], F32) for i in which_list]
                rG = [Res() for _ in which_list]
                hidT = sb(ph, "hidT", [128, 32, 384], BF16)
                rhid = Res()
                with ExitStack() as phg:
                    Gtmp = sb(phg, "Gtmp2", [128, D], F32)
                    rGtmp = Res()
                    gate_rows(l, 5, 3, W2[:, 0:8, :], rW2, G, rG, which_list, Gtmp, rGtmp)
                    S.barrier()
                w2v = w2_d[l].rearrange("(hc p) n -> p hc n", p=128)
                for q in range(8):
                    S.dma('pool', W2[:, q * 4:(q + 1) * 4, :], w2v[:, q * 4:(q + 1) * 4, :], writes=[rW2])
                hT2r = Ring([sb(ph, "hT2_%d" % i, [128, 8, 384], BF16) for i in range(1)])
                w1r = Ring([sb(ph, "w1_%d" % i, [128, 8, 256], BF16) for i in range(3)])
                relu_t = Ring([sb(ph, "relu%d" % i, [128, 384], F32) for i in range(2)])
                blocks = [tiles_upd[i:i + 3] for i in range(0, len(tiles_upd), 3)]
                for blk in blocks:
                    nb = len(blk) * 128
                    hT2, rh2 = hT2r.next()
                    for j, t in enumerate(blk):
                        norm_mod_transpose(t, 1, hT2, rh2, j * 128)
                    for p in range(16):
                        w1, rw1 = w1r.next()
                        S.dma('pool', w1[:], wview(w1_d[l], p * 256, (p + 1) * 256), writes=[rw1])
                        for h2 in range(2):
                            hc = p * 2 + h2
                            bk = pf.alloc()
                            for kc in range(8):
                                S.op('pe', lambda e, kc=kc, bk=bk, h2=h2, w1=w1: e.matmul(pf.t[bk][:, 0:nb], lhsT=w1[:, kc, h2 * 128:(h2 + 1) * 128],
                                                                                   rhs=hT2[:, kc, 0:nb], start=(kc == 0), stop=(kc == 7)),
                                     reads=[rw1, rh2], writes=[pf.res[bk]])
                            rl_, rrl_ = relu_t.next()
                            S.op('act', lambda e, bk=bk, rl_=rl_: e.activation(out=rl_[:, 0:nb], in_=pf.t[bk][:, 0:nb], func=AF.Relu),
                                 reads=[pf.res[bk]], writes=[rrl_])
                            pf.release(bk)
                            S.op('dve', lambda e, hc=hc, rl_=rl_: e.tensor_tensor(out=hidT[:, hc, 0:nb], in0=rl_[:, 0:nb], in1=rl_[:, 0:nb], op=ALU.mult),
                                 reads=[rrl_], writes=[rhid])
                    for j, t in enumerate(blk):
                        ci = 1 if t < 2 else 0
                        banks = [pf.alloc(), pf.alloc()]
                        for c in range(2):
                            for hc in range(32):
                                S.op('pe', lambda e, c=c, hc=hc, j=j: e.matmul(pf.t[banks[c]][:], lhsT=hidT[:, hc, j * 128:(j + 1) * 128],
                                                                            rhs=W2[:, hc, c * 512:(c + 1) * 512], start=(hc == 0), stop=(hc == 31)),
                                     reads=[rhid, rW2], writes=[pf.res[banks[c]]])
                        residual_epilogue(t, banks, G[ci], rG[ci])
                        pf.release(banks[0])
                        pf.release(banks[1])
            if debug_stop == (l, 'mlp'):
                return True

        for l in layers:
            if run_layer(l):
                break
        S.barrier()
        yv = y_d.rearrange("(t p) d -> p t d", p=128)
        ro = Res()
        if out_all:
            S.dma('sp', yv[:, 0:2, :], X[:, 0:2, :], reads=Xr[0:2], writes=[ro])
        yo = 2 if out_all else 0
        for i in range(4):
            S.dma('sp', yv[:, yo + 4 * i:yo + 4 * i + 4, :], X[:, 2 + 4 * i:6 + 4 * i, :], reads=Xr[2 + 4 * i:6 + 4 * i], writes=[ro])
        S.barrier()
    return nc


def _host_constants():
    ident = np.eye(128, dtype=np.float32)
    t = np.arange(T)
    pos = np.stack([t // 64, t % 64], -1).astype(np.float32)
    inv = (np.float32(10000.0) ** (-np.arange(16, dtype=np.float32) / np.float32(16))).astype(np.float32)
    ang = pos[..., None] * inv
    cos = np.cos(ang).astype(np.float32)
    sin = np.sin(ang).astype(np.float32)
    C = np.stack([cos, cos], axis=2)
    Sg = np.stack([-sin, sin], axis=2)
    cos_t = np.ascontiguousarray(C.reshape(16, 128, 64).transpose(1, 0, 2))
    sin_t = np.ascontiguousarray(Sg.reshape(16, 128, 64).transpose(1, 0, 2))
    a = np.arange(128)[:, None]
    b = np.arange(128)[None, :]
    mprev = (b <= a).astype(np.float32)
    mnext = (a <= b).astype(np.float32)
    return dict(ident=ident, cos_t=cos_t, sin_t=sin_t, mprev=mprev, mnext=mnext)


def _tbl_from_rpb(rpb):
    kc = (np.arange(128) % 64)[:, None]
    qc = np.arange(64)[None, :]
    cs = np.clip(qc - 8, 0, 48)
    valid = (kc >= cs) & (kc < cs + 16)
    dc = np.clip(kc - qc, -15, 15) + 15
    ext = np.concatenate([rpb.reshape(8, -1), np.full((8, 1), NEGV, np.float32)], axis=1)
    idx = np.empty((128, 16, 64), np.int64)
    for j in range(15):
        idx[:, j, :] = np.where(valid, j * 31 + dc, 465)
    idx[:, 15, :] = 465
    tbl = ext[:, idx]
    return np.ascontiguousarray(tbl.transpose(1, 0, 2, 3)).astype(np.float32)


_CACHE = {}
_MODE = "fused"


def kernel(x, c, ctx, c_ctx, w_mod, b_mod, norm_g, w_in, w_out, w_mlp_in, w_mlp_out,
           qk_norm_a, diff_lambda, diff_subln, sink_c, rpb_d, _debug_stop=None):
    f = lambda a: np.ascontiguousarray(np.asarray(a, dtype=np.float32))
    x, c, ctx, c_ctx = f(x), f(c), f(ctx), f(c_ctx)
    w_mod, b_mod, norm_g, w_in, w_out, w_mlp_in, w_mlp_out = map(f, (w_mod, b_mod, norm_g, w_in, w_out, w_mlp_in, w_mlp_out))
    qk_norm_a, diff_lambda, diff_subln, sink_c, rpb_d = map(f, (qk_norm_a, diff_lambda, diff_subln, sink_c, rpb_d))
    def get_nc(layers):
        key = ("nc", layers, _debug_stop)
        if key not in _CACHE:
            _CACHE[key] = build_program(layers, _debug_stop)
        return _CACHE[key]
    consts = _host_constants()
    shared = dict(
        w_mod=w_mod, b_mod=b_mod, norm_g=norm_g, w_in=w_in, w_out=w_out, w_mlp_in=w_mlp_in, w_mlp_out=w_mlp_out,
        bmod_col=f(b_mod.reshape(2, 48, 128).transpose(2, 0, 1)),
        ng_col=f(norm_g.reshape(2, 4, 8, 128).transpose(3, 0, 1, 2)),
        qkg=f(qk_norm_a[0].reshape(1, 128)), lamp=f(diff_lambda[0].reshape(1, 256)),
        subln_col=f(diff_subln[0].reshape(128, 1)), sink=f(sink_c[0].reshape(1, 8)),
        tbl=_tbl_from_rpb(rpb_d[0]), **consts)
    def launch(layers, xs, cs):
        in_maps = []
        for b in range(8):
            m = dict(shared)
            m["x"] = f(xs[b])
            m["ctx"] = f(cs[b])
            m["ccol"] = f(np.stack([c[b].reshape(8, 128).T, c_ctx.reshape(8, 128).T], -1))
            in_maps.append(m)
        res = run_bass_kernel_spmd(get_nc(layers), in_maps, core_ids=list(range(8)))
        return [r["y"] for r in res.results]

    if _debug_stop is not None:
        ys = launch((_debug_stop[0],), x, ctx)
        return np.stack(ys, 0).astype(np.float32)
    if _MODE == "fused":
        ys = launch((0, 1), x, ctx)
        return np.stack(ys, 0).astype(np.float32)
    ys = launch((0,), x, ctx)
    ys = launch((1,), [y[LC:] for y in ys], [y[:LC] for y in ys])
    return np.stack(ys, 0).astype(np.float32)
```
